# Optimizing a Trainium2 kernel written in Bass

```python
import math
import jax, jax.numpy as jnp
from jax import lax
import numpy as np

D_MODEL = 1024
BATCH = 8
SEQ = 2048
DEPTH = 1
DEC_BATCH = 128
DEC_SEQ = 8
PAST_LEN = 16384
PAGE_SIZE = 128

D_MIX = D_MODEL
D_GLA = D_MIX // 2
GLA_HEADS = 4
GLA_DV = D_GLA // GLA_HEADS
GLA_DK = GLA_DV // 2
GLA_KDIM = GLA_HEADS * GLA_DK
GLA_GATE_RANK = 16
GLA_GATE_NORM = 16.0
GLA_CHUNK = 64
D_S5 = D_MIX - D_GLA
S5_GROUP = 16
S5_GROUPS = D_S5 // S5_GROUP
S5_STATE = 64
N_META = 16
D_FF = ((-(-(8 * D_MODEL) // 3) + 255) // 256) * 256
D_IN = 2 * GLA_KDIM + 2 * D_GLA + GLA_GATE_RANK + D_S5
EPS = 1e-6

kernel_name = "hymba_gla_s5_sandwich_step"


def rms_norm(x, g):
    xf = x.astype(jnp.float32)
    y = xf * lax.rsqrt(jnp.mean(xf * xf, axis=-1, keepdims=True) + EPS)
    return (y * g.astype(jnp.float32)).astype(x.dtype)


def gla_chunked(q, k, v, lg, s0, chunk):
    bsz, length, nh, dk = q.shape
    dv = v.shape[-1]
    nc = length // chunk

    def to_chunks(t):
        return t.reshape(bsz, nc, chunk, nh, t.shape[-1]).transpose(1, 0, 3, 2, 4)

    qc, kc, vc, gc = to_chunks(q), to_chunks(k), to_chunks(v), to_chunks(lg)
    b = jnp.cumsum(gc, axis=3)
    b_last = b[:, :, :, -1:, :]
    q_dec = qc * jnp.exp(b)
    k_inv = kc * jnp.exp(-b)
    k_end = kc * jnp.exp(b_last - b)
    causal = jnp.tril(jnp.ones((chunk, chunk), dtype=bool))

    def step(S, xs):
        q_i, k_i, ke_i, v_i, bl_i = xs
        att = jnp.where(causal, jnp.einsum('bhtd,bhsd->bhts', q_i, k_i), 0.0)
        o = jnp.einsum('bhts,bhsv->bhtv', att, v_i) + jnp.einsum('bhtd,bhdv->bhtv', q_i, S)
        S = jnp.exp(bl_i[..., 0, :])[..., None] * S + jnp.einsum('bhsd,bhsv->bhdv', ke_i, v_i)
        return S, o

    S, o = lax.scan(step, s0, (q_dec, k_inv, k_end, vc, b_last))
    o = o.transpose(1, 0, 3, 2, 4).reshape(bsz, length, nh, dv)
    return o, S


def s5_scan(u, h0_re, h0_im, a_re, a_im, b_re, b_im, c_re, c_im, d_skip, log_dt):
    f32 = jnp.float32
    bsz, length, _ = u.shape
    uf = u.astype(f32)
    ug = uf.reshape(bsz, length, S5_GROUPS, S5_GROUP)
    lam_re = jnp.minimum(a_re.astype(f32), -1e-4)
    lam_im = a_im.astype(f32)
    dt = jnp.exp(log_dt.astype(f32))[:, None]
    mag = jnp.exp(lam_re * dt)
    abar_re = mag * jnp.cos(lam_im * dt)
    abar_im = mag * jnp.sin(lam_im * dt)
    den = lam_re * lam_re + lam_im * lam_im
    nr, ni = abar_re - 1.0, abar_im
    f_re = (nr * lam_re + ni * lam_im) / den
    f_im = (ni * lam_re - nr * lam_im) / den
    br, bi = b_re.astype(f32), b_im.astype(f32)
    bb_re = f_re[..., None] * br - f_im[..., None] * bi
    bb_im = f_re[..., None] * bi + f_im[..., None] * br
    bu_re = jnp.einsum('blgj,gnj->blgn', ug, bb_re)
    bu_im = jnp.einsum('blgj,gnj->blgn', ug, bb_im)
    a_r = jnp.broadcast_to(abar_re, bu_re.shape)
    a_i = jnp.broadcast_to(abar_im, bu_re.shape)

    def combine(e1, e2):
        a1r, a1i, b1r, b1i = e1
        a2r, a2i, b2r, b2i = e2
        return (a2r * a1r - a2i * a1i,
                a2r * a1i + a2i * a1r,
                a2r * b1r - a2i * b1i + b2r,
                a2r * b1i + a2i * b1r + b2i)

    pr, pi, hr, hi = lax.associative_scan(combine, (a_r, a_i, bu_re, bu_im), axis=1)
    h0r = h0_re.astype(f32)[:, None]
    h0i = h0_im.astype(f32)[:, None]
    h_re = pr * h0r - pi * h0i + hr
    h_im = pr * h0i + pi * h0r + hi
    y = (jnp.einsum('gjn,blgn->blgj', c_re.astype(f32), h_re)
         - jnp.einsum('gjn,blgn->blgj', c_im.astype(f32), h_im))
    y = y.reshape(bsz, length, D_S5) + d_skip.astype(f32) * uf
    return y, h_re[:, -1], h_im[:, -1]


def hybrid_layer(x, s_gla0, s5r0, s5i0, segments,
                 g_pre_mix, w_in, w_gk2, b_gk, gla_norm,
                 s5_a_re, s5_a_im, s5_b_re, s5_b_im, s5_c_re, s5_c_im, s5_d, s5_log_dt,
                 w_s5_glu, s5_norm, w_o, g_post_mix, g_pre_ffn, w_gate, w_up, w_down, g_post_ffn):
    f32 = jnp.float32
    bsz, length, _ = x.shape
    h = rms_norm(x, g_pre_mix)
    proj = h @ w_in
    cuts = [GLA_KDIM, 2 * GLA_KDIM, 2 * GLA_KDIM + D_GLA, 2 * GLA_KDIM + 2 * D_GLA,
            2 * GLA_KDIM + 2 * D_GLA + GLA_GATE_RANK]
    q, k, v, g, gk_lr, u = jnp.split(proj, cuts, axis=-1)

    lg = jax.nn.log_sigmoid((gk_lr @ w_gk2 + b_gk).astype(f32)) / GLA_GATE_NORM
    qh = q.astype(f32).reshape(bsz, length, GLA_HEADS, GLA_DK) * (GLA_DK ** -0.5)
    kh = k.astype(f32).reshape(bsz, length, GLA_HEADS, GLA_DK)
    vh = v.astype(f32).reshape(bsz, length, GLA_HEADS, GLA_DV)
    lgh = lg.reshape(bsz, length, GLA_HEADS, GLA_DK)
    S = s_gla0.astype(f32)
    outs = []
    start = 0
    for seg_len, chunk in segments:
        o, S = gla_chunked(qh[:, start:start + seg_len], kh[:, start:start + seg_len],
                           vh[:, start:start + seg_len], lgh[:, start:start + seg_len], S, chunk)
        outs.append(o)
        start += seg_len
    o_gla = jnp.concatenate(outs, axis=1)
    o_gla = rms_norm(o_gla, gla_norm).reshape(bsz, length, D_GLA)
    o_gla = (o_gla * jax.nn.silu(g.astype(f32))).astype(x.dtype)

    y5, h_re, h_im = s5_scan(u, s5r0, s5i0, s5_a_re, s5_a_im, s5_b_re, s5_b_im,
                             s5_c_re, s5_c_im, s5_d, s5_log_dt)
    y5 = jax.nn.gelu(y5)
    y5 = y5 * jax.nn.sigmoid(y5 @ w_s5_glu.astype(f32))
    y5 = rms_norm(y5, s5_norm).astype(x.dtype)

    mix = jnp.concatenate([o_gla, y5], axis=-1) @ w_o
    x = x + rms_norm(mix, g_post_mix)

    h = rms_norm(x, g_pre_ffn)
    f = (jax.nn.silu(h @ w_gate) * (h @ w_up)) @ w_down
    x = x + rms_norm(f, g_post_ffn)
    return x, S, h_re, h_im


def setup_inputs(seed: int = 0) -> dict:
    key = jax.random.key(seed)
    ks = iter(jax.random.split(key, 40))
    nrm = lambda shape, scale: jax.random.normal(next(ks), shape, jnp.float32) * scale
    gain = lambda shape: 1.0 + nrm(shape, 0.01)
    L = DEPTH
    n_idx = jnp.arange(S5_STATE, dtype=jnp.float32)
    return {
        "x_prompt": nrm((BATCH, SEQ, D_MODEL), 1.0),
        "x_sample": nrm((DEC_BATCH, DEC_SEQ, D_MODEL), 1.0),
        "state_gla": nrm((L, DEC_BATCH, GLA_HEADS, GLA_DK, GLA_DV), 0.3),
        "state_s5_re": nrm((L, DEC_BATCH, S5_GROUPS, S5_STATE), 0.1),
        "state_s5_im": nrm((L, DEC_BATCH, S5_GROUPS, S5_STATE), 0.1),
        "meta_tokens": nrm((N_META, D_MODEL), 1.0),
        "g_pre_mix": gain((L, D_MODEL)),
        "w_in": nrm((L, D_MODEL, D_IN), D_MODEL ** -0.5),
        "w_gk2": nrm((L, GLA_GATE_RANK, GLA_KDIM), GLA_GATE_RANK ** -0.5),
        "b_gk": nrm((L, GLA_KDIM), 0.01),
        "gla_norm": gain((L, GLA_DV)),
        "s5_a_re": -0.5 + nrm((L, S5_GROUPS, S5_STATE), 0.01),
        "s5_a_im": math.pi * n_idx + nrm((L, S5_GROUPS, S5_STATE), 0.01),
        "s5_b_re": nrm((L, S5_GROUPS, S5_STATE, S5_GROUP), (2.0 * S5_GROUP) ** -0.5),
        "s5_b_im": nrm((L, S5_GROUPS, S5_STATE, S5_GROUP), (2.0 * S5_GROUP) ** -0.5),
        "s5_c_re": nrm((L, S5_GROUPS, S5_GROUP, S5_STATE), (2.0 * S5_STATE) ** -0.5),
        "s5_c_im": nrm((L, S5_GROUPS, S5_GROUP, S5_STATE), (2.0 * S5_STATE) ** -0.5),
        "s5_d": nrm((L, D_S5), 0.5),
        "s5_log_dt": jax.random.uniform(next(ks), (L, S5_GROUPS), jnp.float32,
                                         math.log(1e-3), math.log(1e-1)),
        "w_s5_glu": nrm((L, D_S5, D_S5), D_S5 ** -0.5),
        "s5_norm": gain((L, D_S5)),
        "w_o": nrm((L, D_MIX, D_MODEL), D_MIX ** -0.5),
        "g_post_mix": gain((L, D_MODEL)),
        "g_pre_ffn": gain((L, D_MODEL)),
        "w_gate": nrm((L, D_MODEL, D_FF), D_MODEL ** -0.5),
        "w_up": nrm((L, D_MODEL, D_FF), D_MODEL ** -0.5),
        "w_down": nrm((L, D_FF, D_MODEL), D_FF ** -0.5),
        "g_post_ffn": gain((L, D_MODEL)),
    }


def reference(x_prompt, x_sample, state_gla, state_s5_re, state_s5_im, meta_tokens,
              g_pre_mix, w_in, w_gk2, b_gk, gla_norm,
              s5_a_re, s5_a_im, s5_b_re, s5_b_im, s5_c_re, s5_c_im, s5_d, s5_log_dt,
              w_s5_glu, s5_norm, w_o, g_post_mix, g_pre_ffn, w_gate, w_up, w_down, g_post_ffn):
    bp, seq_p, _ = x_prompt.shape
    bs, seq_s, _ = x_sample.shape
    meta = jnp.broadcast_to(meta_tokens.astype(x_prompt.dtype)[None], (bp, N_META, D_MODEL))
    xp = jnp.concatenate([meta, x_prompt], axis=1)
    xs = x_sample
    seg_prompt = ((N_META, N_META), (seq_p, GLA_CHUNK))
    seg_sample = ((seq_s, seq_s),)
    zeros_gla = jnp.zeros((bp, GLA_HEADS, GLA_DK, GLA_DV), jnp.float32)
    zeros_s5 = jnp.zeros((bp, S5_GROUPS, S5_STATE), jnp.float32)
    gp_l, rp_l, ip_l, gs_l, rs_l, is_l = [], [], [], [], [], []
    for l in range(DEPTH):
        w = (g_pre_mix[l], w_in[l], w_gk2[l], b_gk[l], gla_norm[l],
             s5_a_re[l], s5_a_im[l], s5_b_re[l], s5_b_im[l], s5_c_re[l], s5_c_im[l],
             s5_d[l], s5_log_dt[l], w_s5_glu[l], s5_norm[l], w_o[l], g_post_mix[l],
             g_pre_ffn[l], w_gate[l], w_up[l], w_down[l], g_post_ffn[l])
        xp, sg, sr, si = hybrid_layer(xp, zeros_gla, zeros_s5, zeros_s5, seg_prompt, *w)
        gp_l.append(sg); rp_l.append(sr); ip_l.append(si)
        xs, sg, sr, si = hybrid_layer(xs, state_gla[l], state_s5_re[l], state_s5_im[l],
                                      seg_sample, *w)
        gs_l.append(sg); rs_l.append(sr); is_l.append(si)
    y_prompt = xp[:, N_META:]
    y_sample = xs
    new_gla_prompt = jnp.stack(gp_l)
    new_s5_re_prompt = jnp.stack(rp_l)
    new_s5_im_prompt = jnp.stack(ip_l)
    new_gla_sample = jnp.stack(gs_l)
    new_s5_re_sample = jnp.stack(rs_l)
    new_s5_im_sample = jnp.stack(is_l)
    return (y_prompt, y_sample, new_gla_prompt, new_s5_re_prompt, new_s5_im_prompt,
            new_gla_sample, new_s5_re_sample, new_s5_im_sample)
```

```python
import contextlib
import math
import os
import numpy as np
import concourse.bass as bass
import concourse.mybir as mybir
from concourse.bass_utils import run_bass_kernel_spmd

F32 = mybir.dt.float32
BF16 = mybir.dt.bfloat16
I32 = mybir.dt.int32
AF = mybir.ActivationFunctionType
ALU = mybir.AluOpType

D = 1024
DIN = 2064
DFF = 2816
NFF = 22
SEQ = 2048
NMETA = 16
T5 = 4
EPS = 1e-6
TWO_PI = 2.0 * math.pi
TWO_PI_HI = float(np.float32(TWO_PI))
TWO_PI_LO = TWO_PI - TWO_PI_HI
NCORES = 8


class Buf:
    def __init__(self, name):
        self.name = name
        self.last_w = None
        self.readers = []
        self.chan = None


class Sched:
    ENGS = ["pe", "act", "dve", "pool", "sp"]

    def __init__(self, nc):
        self.nc = nc
        self.ops = {e: [] for e in self.ENGS}
        self.count = {e: 0 for e in self.ENGS}
        self.seen = {e: {} for e in self.ENGS}
        self.nchan = 0
        self.chan_count = {}

    def _deps(self, eng, reads, writes):
        deps = []
        for b in reads:
            if b.last_w is not None:
                deps.append(b.last_w)
        for b in writes:
            if b.last_w is not None:
                deps.append(b.last_w)
            deps.extend(b.readers)
        seen = self.seen[eng]
        d = {}
        for (k, v) in deps:
            if k == "pe" and eng == "pe":
                continue
            if seen.get(k, 0) >= v:
                continue
            d[k] = max(d.get(k, 0), v)
        for k, v in d.items():
            seen[k] = v
        return list(d.items())

    def _commit(self, ev, reads, writes):
        for b in writes:
            b.last_w = ev
            b.readers = []
        for b in reads:
            if b not in writes:
                b.readers.append(ev)
                if len(b.readers) > 64:
                    dd = {}
                    for (k, v) in b.readers:
                        dd[k] = max(dd.get(k, 0), v)
                    b.readers = list(dd.items())

    def op(self, eng, fn, reads=(), writes=(), selfwait=False):
        waits = self._deps(eng, reads, writes)
        if selfwait and self.count[eng] > 0:
            waits = [w for w in waits if w[0] != eng] + [(eng, self.count[eng])]
        self.count[eng] += 1
        ev = (eng, self.count[eng])
        self.ops[eng].append((waits, fn, (eng, 1)))
        self._commit(ev, reads, writes)
        return ev

    def dma(self, eng, fn, reads=(), writes=(), chan=None):
        waits = self._deps(eng, reads, writes)
        if chan.chan is None:
            chan.chan = "dma%d" % self.nchan
            self.nchan += 1
            self.chan_count[chan.chan] = 0
        self.chan_count[chan.chan] += 16
        ev = (chan.chan, self.chan_count[chan.chan])
        self.ops[eng].append((waits, fn, (chan.chan, 16)))
        self._commit(ev, reads, writes)
        return ev

    def barrier(self):
        evs = [(e, self.count[e]) for e in self.ENGS if self.count[e] > 0]
        evs += [(k, v) for k, v in self.chan_count.items() if v > 0]
        for e in self.ENGS:
            waits = []
            for (k, v) in evs:
                if k == e:
                    continue
                if self.seen[e].get(k, 0) >= v:
                    continue
                self.seen[e][k] = v
                waits.append((k, v))
            if waits:
                self.ops[e].append((waits, None, None))

    def emit(self):
        nc = self.nc
        with contextlib.ExitStack() as es:
            sems = {}
            for e in self.ENGS:
                sems[e] = es.enter_context(nc.semaphore("s_" + e))
            for k in self.chan_count:
                sems[k] = es.enter_context(nc.semaphore("s_" + k))
            block = es.enter_context(nc.Block())
            engmap = {"pe": block.tensor, "act": block.scalar, "dve": block.vector,
                      "pool": block.gpsimd, "sp": block.sync}

            def mk(ename):
                def body(eng):
                    for (waits, fn, inc) in self.ops[ename]:
                        for (k, v) in waits:
                            eng.wait_ge(sems[k], v)
                        if fn is not None:
                            ins = fn(eng)
                            ins.then_inc(sems[inc[0]], inc[1])
                return body
            for e in self.ENGS:
                if self.ops[e]:
                    engmap[e](mk(e))


def _resh(v, shape):
    if len(shape) == 1:
        return v
    names = ["d%d" % i for i in range(len(shape))]
    s = "p (" + " ".join(names) + ") -> p " + " ".join(names)
    kw = {names[i]: int(shape[i]) for i in range(len(shape))}
    return v.rearrange(s, **kw)


class Arena:
    def __init__(self, ap, nwords):
        self.ap = ap
        self.n = nwords
        self.off = 0
        self.hi = 0

    def mark(self):
        return self.off

    def reset(self, m):
        self.off = m

    def alloc(self, shape, dtype, parts=128, name=None):
        esz = {F32: 4, BF16: 2, I32: 4}[dtype]
        nelem = int(np.prod(shape))
        nw = (nelem * esz + 3) // 4
        nw = (nw + 7) // 8 * 8
        o = self.off
        self.off += nw
        self.hi = max(self.hi, self.off)
        assert self.off <= self.n, "arena overflow %s %d > %d" % (name, self.off * 4, self.n * 4)
        v = self.ap[0:parts, o:o + (nelem * esz + 3) // 4]
        if dtype != F32:
            v = v.bitcast(dtype)
            v = v[:, 0:nelem]
        return _resh(v, list(shape))


def build_nc():
    nc = bass.Bass("TRN2", target_bir_lowering=False)

    def din(name, shape):
        return nc.dram_tensor(name, list(shape), F32, kind="ExternalInput").ap()

    def dout(name, shape):
        return nc.dram_tensor(name, list(shape), F32, kind="ExternalOutput").ap()

    xp = din("xp", [SEQ, D]); meta = din("meta", [NMETA, D]); xs = din("xs", [128, D])
    sgla = din("sgla", [16, 4, 64, 128]); s5re0 = din("s5re0", [16, 2048]); s5im0 = din("s5im0", [16, 2048])
    g_pre_mix = din("g_pre_mix", [D]); w_in = din("w_in", [D, DIN]); w_gk2 = din("w_gk2", [16, 256])
    b_gk = din("b_gk", [1, 256]); gla_norm = din("gla_norm", [128, 1])
    a_re = din("s5_a_re", [2048]); a_im = din("s5_a_im", [2048])
    b_re = din("s5_b_re", [2048, 16]); b_im = din("s5_b_im", [2048, 16])
    c_re = din("s5_c_re", [32, 16, 64]); c_im = din("s5_c_im", [32, 16, 64])
    s5_d = din("s5_d", [512]); log_dt = din("s5_log_dt", [1, 32])
    w_glu = din("w_s5_glu", [512, 512]); s5_norm = din("s5_norm", [512])
    w_o = din("w_o", [D, D]); g_post_mix = din("g_post_mix", [1, D]); g_pre_ffn = din("g_pre_ffn", [D])
    w_gate = din("w_gate", [D, DFF]); w_up = din("w_up", [D, DFF]); w_down = din("w_down", [DFF, D])
    g_post_ffn = din("g_post_ffn", [1, D])

    yp = dout("yp", [SEQ, D]); ys = dout("ys", [128, D]); glap = dout("glap", [4, 64, 128])
    s5rep = dout("s5rep", [16, 128]); s5imp = dout("s5imp", [16, 128])
    glas = dout("glas", [16, 4, 64, 128]); s5res = dout("s5res", [16, 2048]); s5ims = dout("s5ims", [16, 2048])
    x1d = nc.dram_tensor("x1_scratch", [SEQ + 128, D], F32, kind="Internal").ap()
    wg_b = nc.dram_tensor("wg_bf16", [D, DFF], BF16, kind="Internal").ap()
    wu_b = nc.dram_tensor("wu_bf16", [D, DFF], BF16, kind="Internal").ap()
    wd_b = nc.dram_tensor("wd_bf16", [DFF, D], BF16, kind="Internal").ap()

    S = Sched(nc)
    es = contextlib.ExitStack()
    NW = 53000
    arena_t = es.enter_context(nc.sbuf_tensor("arena", [128, NW], F32))
    AR = Arena(arena_t, NW)
    banks = [es.enter_context(nc.psum_tensor("bank%d" % i, [128, 512], F32)) for i in range(8)]
    bank_bufs = [Buf("bank%d" % i) for i in range(8)]
    bank_rr = [0]

    def pbank():
        i = bank_rr[0] % 7
        bank_rr[0] += 1
        return banks[i][:, :], bank_bufs[i]

    pe_last_bases = [frozenset([0])]

    def pe(mms, reads, writes, skip=False, selfwait=False):
        bases = frozenset(int(m[1].base_partition()) for m in mms)
        assert len(bases) == 1, "mixed PE row-tile positions in one group"
        if bases != pe_last_bases[0]:
            selfwait = True
        pe_last_bases[0] = bases
        def fn(e, mms=mms, skip=skip):
            ins = None
            for (o, l, r, st, sp_) in mms:
                if skip:
                    ins = e.matmul(o, lhsT=l, rhs=r, start=st, stop=sp_, skip_group_check=True)
                else:
                    ins = e.matmul(o, lhsT=l, rhs=r, start=st, stop=sp_)
            return ins
        return S.op("pe", fn, reads, writes, selfwait=selfwait)

    def act(out, in_, func, reads, writes, bias=None, scale=None, accum=None):
        kw = {}
        if bias is not None:
            kw["bias"] = bias
        if scale is not None:
            kw["scale"] = scale
        if accum is not None:
            kw["accum_out"] = accum
        return S.op("act", lambda e: e.activation(out, in_, func, **kw), reads, writes)

    def tt(eng, out, a, b, op, reads, writes):
        return S.op(eng, lambda e: e.tensor_tensor(out, a, b, op), reads, writes)

    def ts(eng, out, a, s1, s2, op0, op1, reads, writes):
        if s2 is None:
            return S.op(eng, lambda e: e.tensor_scalar(out, a, s1, None, op0), reads, writes)
        return S.op(eng, lambda e: e.tensor_scalar(out, a, s1, s2, op0, op1), reads, writes)

    def stt(eng, out, in0, sc, in1, op0, op1, reads, writes):
        return S.op(eng, lambda e: e.scalar_tensor_tensor(out, in0, sc, in1, op0, op1), reads, writes)

    def cp(eng, out, in_, reads, writes):
        if eng == "act":
            return S.op("act", lambda e: e.copy(out, in_), reads, writes)
        return S.op(eng, lambda e: e.tensor_copy(out, in_), reads, writes)

    def memset(eng, ap, val, writes):
        return S.op(eng, lambda e: e.memset(ap, val), (), writes)

    def dma(eng, out, in_, reads, writes, chan, slow=False):
        if slow:
            return S.dma(eng, lambda e: e.dma_start(out=out, in_=in_, allow_slow_non_contiguous=True),
                         reads, writes, chan)
        return S.dma(eng, lambda e: e.dma_start(out=out, in_=in_), reads, writes, chan)

    def asel(ap, pattern, base, cm, buf, op=ALU.is_ge, fill=0.0):
        return S.op("pool", lambda e: e.affine_select(out=ap, in_=ap, pattern=pattern, compare_op=op,
                                                     fill=fill, base=base, channel_multiplier=cm),
                    [buf], [buf])

    MV = [0, 1, 2, 3, 4, 8, 12, 16, 20, 24, 28, 32]
    NM = len(MV)
    identf = AR.alloc([128], F32); b_identf = Buf("identf")
    identb = AR.alloc([128], BF16); b_identb = Buf("identb")
    onesb = AR.alloc([128], BF16); b_onesb = Buf("onesb")
    masks = {}
    for ty in ("p", "s"):
        masks[ty] = dict(
            cs=AR.alloc([128], F32), rev=AR.alloc([128], F32), att=AR.alloc([128], F32),
            col=AR.alloc([16], F32), col1=AR.alloc([16], F32), buf=Buf("mask" + ty))
    gpm_row = AR.alloc([D], F32); b_gpm_row = Buf("gpm_row")
    gpm_col = AR.alloc([8], F32); gpf_col = AR.alloc([8], F32); b_gcol = Buf("gcol")
    glan = AR.alloc([1], F32); dsk = AR.alloc([4], F32); s5n = AR.alloc([4], F32); b_vec = Buf("vec")
    wgk = AR.alloc([256], F32, parts=32); b_wgk = Buf("wgk")
    APWrr = AR.alloc([8, 16, 2], F32); APWii = AR.alloc([8, 16, 2], F32); b_A8 = Buf("A8")
    Sg = AR.alloc([4, 128], F32, parts=64); b_Sg = Buf("Sg")
    Sgb = AR.alloc([4, 128], BF16, parts=64); b_Sgb = Buf("Sgb")
    Hcar = AR.alloc([16, 2], F32); b_Hcar = Buf("Hcar")
    eps_t = AR.alloc([1], F32); b_eps = Buf("eps")
    PERSIST_MARK = AR.mark()
    BPT = AR.alloc([4, 2, 2, T5, 128], BF16); b_BPT = Buf("BPT")
    CAT = AR.alloc([16, T5, 2, 64], BF16); b_CAT = Buf("CAT")
    KT = AR.alloc([4, T5, 128], BF16); b_KT = Buf("KT")
    win = AR.alloc([8, DIN], BF16); b_win = Buf("win")
    wglu = AR.alloc([4, 512], BF16); b_wglu = Buf("wglu")
    wo = AR.alloc([8, D], BF16); b_wo = Buf("wo")
    W_END = AR.mark()

    memset("pool", identf, 1.0, [b_identf])
    asel(identf, [[-1, 128]], 0, 1, b_identf, op=ALU.is_equal)
    cp("pool", identb, identf, [b_identf], [b_identb])
    memset("pool", onesb, 1.0, [b_onesb])
    memset("pool", Sg, 0.0, [b_Sg])
    memset("pool", Sgb, 0.0, [b_Sgb])
    memset("pool", Hcar, 0.0, [b_Hcar])
    memset("pool", eps_t, EPS, [b_eps])

    def build_masks(ty, cs_):
        m = masks[ty]
        mb = m["buf"]
        ncn = 128 // cs_
        for key, val in (("cs", -1.0 / 16), ("rev", -1.0 / 16), ("att", 1.0)):
            ap = m[key]
            memset("pool", ap, val, [mb])
            v3 = ap.rearrange("p (c i) -> p c i", c=ncn)
            asel(v3, [[-cs_, ncn], [0, cs_]], 0, 1, mb)
            asel(v3, [[cs_, ncn], [0, cs_]], cs_ - 1, -1, mb)
            if key == "rev":
                asel(ap, [[-1, 128]], -1, 1, mb)
            else:
                asel(ap, [[1, 128]], 0, -1, mb)
        nci = min(ncn, 16)
        for key, val in (("col", -1.0 / 16), ("col1", 1.0)):
            ap = m[key]
            memset("pool", ap, val, [mb])
            asel(ap[:, 0:nci], [[-cs_, nci]], 0, 1, mb)
            asel(ap[:, 0:nci], [[cs_, nci]], cs_ - 1, -1, mb)


    stg = AR.alloc([768], F32, parts=16); b_stg = Buf("stg")
    memset("pool", stg, 0.0, [b_stg])
    dma("sp", stg[0:8, 0:128], g_pre_mix.rearrange("(k p) -> k p", p=128), [], [b_stg], b_stg)
    dma("sp", stg[0:8, 128:256], g_pre_ffn.rearrange("(k p) -> k p", p=128), [], [b_stg], b_stg)
    dma("sp", stg[0:4, 256:384], s5_d.rearrange("(k p) -> k p", p=128), [], [b_stg], b_stg)
    dma("sp", stg[0:4, 384:512], s5_norm.rearrange("(k p) -> k p", p=128), [], [b_stg], b_stg)
    dma("sp", stg[0:16, 512:640], a_re.rearrange("(k p) -> k p", p=128), [], [b_stg], b_stg)
    dma("sp", stg[0:16, 640:768], a_im.rearrange("(k p) -> k p", p=128), [], [b_stg], b_stg)
    dma("sp", glan, gla_norm, [], [b_vec], b_vec)
    bk0, bb0 = pbank()
    pe([(bk0[:, 16 * j:16 * j + 16], stg[0:16, 128 * j:128 * j + 128], identf[0:16, 0:16], True, True)
        for j in range(6)], [b_stg, b_identf], [bb0])
    cp("dve", gpm_col, bk0[:, 0:8], [bb0], [b_gcol])
    cp("dve", gpf_col, bk0[:, 16:24], [bb0], [b_gcol])
    cp("dve", dsk, bk0[:, 32:36], [bb0], [b_vec])
    cp("dve", s5n, bk0[:, 48:52], [bb0], [b_vec])
    dma("sp", wgk[0:16, :], w_gk2, [], [b_wgk], b_wgk)
    dma("sp", wgk[16:17, :], b_gk, [], [b_wgk], b_wgk)
    dma("sp", gpm_row, g_post_mix.partition_broadcast(128), [], [b_gpm_row], b_gpm_row)

    wst = [AR.alloc([DIN], F32) for _ in range(3)]; b_wst = [Buf("wst%d" % i) for i in range(3)]
    wst_i = [0]
    cast_engs = ["act", "dve", "act"]

    def load_cast(dst, b_dst, src_rows, ncols, scale_col=None):
        i = wst_i[0] % 3
        wst_i[0] += 1
        st_, bst = wst[i], b_wst[i]
        dma("sp", st_[:, 0:ncols], src_rows, [], [bst], bst)
        eng = cast_engs[i]
        if scale_col is None:
            cp(eng, dst, st_[:, 0:ncols], [bst], [b_dst])
        elif eng == "act":
            S.op("act", lambda e: e.mul(dst, st_[:, 0:ncols], scale_col), [bst, b_gcol], [b_dst])
        else:
            ts(eng, dst, st_[:, 0:ncols], scale_col, None, ALU.mult, None, [bst, b_gcol], [b_dst])

    b_tb = Buf("tb")
    are = AR.alloc([16], F32); aim = AR.alloc([16], F32); ldt_all = AR.alloc([32], F32)
    dtt = AR.alloc([16], F32); lr = AR.alloc([16], F32); lrdt = AR.alloc([16], F32); th = AR.alloc([16], F32)
    mvals = AR.alloc([NM, 16], F32); MAG = AR.alloc([NM, 16], F32); ANG = AR.alloc([NM, 16], F32)
    RS = AR.alloc([NM, 16], F32); RC = AR.alloc([NM, 16], F32); KF = AR.alloc([NM, 16], F32)
    KI = AR.alloc([NM, 16], I32); PWre = AR.alloc([NM, 16], F32); PWim = AR.alloc([NM, 16], F32)
    t16a = AR.alloc([16], F32); t16b = AR.alloc([16], F32); t16c = AR.alloc([16], F32)
    fre = AR.alloc([16], F32); fim = AR.alloc([16], F32)
    Bre = AR.alloc([16, 16], F32); Bim = AR.alloc([16, 16], F32)
    bbre = AR.alloc([16, 16], F32); bbim = AR.alloc([16, 16], F32); t256 = AR.alloc([16, 16], F32)
    Cin_re = AR.alloc([2, 128], F32); Cin_im = AR.alloc([2, 128], F32)
    Cre = AR.alloc([16, 16], F32); Cim = AR.alloc([16, 16], F32)
    ABre = AR.alloc([T5, 16, 16], F32); ABim = AR.alloc([T5, 16, 16], F32)
    CAre = AR.alloc([T5 + 1, 16, 16], F32); CAimN = AR.alloc([T5 + 1, 16, 16], F32); tCA = AR.alloc([T5 + 1, 16, 16], F32)
    tAB = tCA[:, 0:T5, :, :]
    mask2 = AR.alloc([2], F32); parm = AR.alloc([2], F32); rowm = AR.alloc([8], F32); tmask = AR.alloc([2], F32)
    Xb = [AR.alloc([16, 2, 16], F32) for _ in range(2)]; b_Xb = [Buf("Xb0"), Buf("Xb1")]
    bbpad_re = AR.alloc([16, 128], F32); bbpad_im = AR.alloc([16, 128], F32); b_bbpad = Buf("bbpad")

    cp("dve", are, bk0[:, 64:80], [bb0], [b_tb])
    cp("dve", aim, bk0[:, 80:96], [bb0], [b_tb])
    dma("sp", ldt_all, log_dt.partition_broadcast(128), [], [b_tb], b_tb)
    dma("sp", Bre, b_re.rearrange("(r p) j -> p r j", p=128), [], [b_tb], b_tb)
    dma("act", Bim, b_im.rearrange("(r p) j -> p r j", p=128), [], [b_tb], b_tb)
    for (cin, csrc) in ((Cin_re, c_re), (Cin_im, c_im)):
        cv = csrc.rearrange("(blk prl g2) j n -> blk g2 prl j n", blk=2, prl=8, g2=2)
        for blk in range(2):
            for g2_ in range(2):
                dma("sp", cin[:, blk, g2_ * 64:(g2_ + 1) * 64], cv[blk, g2_], [], [b_tb], b_tb)

    for kt in range(8):
        load_cast(win[:, kt, :], b_win, w_in[kt * 128:(kt + 1) * 128, :], DIN, gpm_col[:, kt:kt + 1])
    for kt in range(4):
        load_cast(wglu[:, kt, :], b_wglu, w_glu[kt * 128:(kt + 1) * 128, :], 512)
    for kt in range(8):
        load_cast(wo[:, kt, :], b_wo, w_o[kt * 128:(kt + 1) * 128, :], D)

    def range_col(ap_col, lo, hi):
        memset("pool", ap_col, 1.0, [b_tb])
        asel(ap_col, [[0, 1]], -lo, 1, b_tb)
        asel(ap_col, [[0, 1]], hi - 1, -1, b_tb)
    range_col(mask2[:, 0:1], 0, 64)
    range_col(mask2[:, 1:2], 64, 128)
    range_col(parm[:, 0:1], 0, 32)
    range_col(tmask[:, 0:1], 64, 96)
    tt("pool", parm[:, 0:1], parm[:, 0:1], tmask[:, 0:1], ALU.add, [b_tb], [b_tb])
    range_col(parm[:, 1:2], 32, 64)
    range_col(tmask[:, 1:2], 96, 128)
    tt("pool", parm[:, 1:2], parm[:, 1:2], tmask[:, 1:2], ALU.add, [b_tb], [b_tb])
    for m_ in range(8):
        range_col(rowm[:, m_:m_ + 1], 16 * m_, 16 * m_ + 16)
    for mi, m_ in enumerate(MV):
        memset("pool", mvals[:, mi, :], float(m_), [b_tb])
    build_masks("p", 64)
    build_masks("s", 8)

    R, W = [b_tb], [b_tb]
    ldv = ldt_all.rearrange("p (r g) -> p r g", g=2)
    cp("dve", dtt[0:64, :], ldv[0:64, :, 0], R, W)
    cp("dve", dtt[64:128, :], ldv[64:128, :, 1], R, W)
    act(dtt, dtt, AF.Exp, R, W)
    ts("dve", lr, are, -1e-4, None, ALU.min, None, R, W)
    tt("dve", lrdt, lr, dtt, ALU.mult, R, W)
    tt("dve", th, aim, dtt, ALU.mult, R, W)
    b9 = lambda a: a.unsqueeze(1).to_broadcast([128, NM, 16])
    tt("dve", MAG, mvals, b9(lrdt), ALU.mult, R, W)
    act(MAG, MAG, AF.Exp, R, W)
    tt("dve", ANG, mvals, b9(th), ALU.mult, R, W)

    def range_reduce(dst, shift):
        ts("dve", dst, ANG, float(shift), None, ALU.add, None, R, W)
        ts("dve", KF, dst, float(1.0 / TWO_PI), None, ALU.mult, None, R, W)
        cp("dve", KI, KF, R, W)
        cp("dve", KF, KI, R, W)
        stt("dve", dst, KF, float(-TWO_PI_HI), dst, ALU.mult, ALU.add, R, W)
        stt("dve", dst, KF, float(-TWO_PI_LO), dst, ALU.mult, ALU.add, R, W)
        ts("dve", KF, dst, float(math.pi), float(-TWO_PI), ALU.is_gt, ALU.mult, R, W)
        tt("dve", dst, dst, KF, ALU.add, R, W)
        ts("dve", dst, dst, float(math.pi), float(-math.pi), ALU.min, ALU.max, R, W)
    range_reduce(RS, TWO_PI)
    range_reduce(RC, TWO_PI + math.pi / 2)
    act(RS, RS, AF.Sin, R, W)
    act(RC, RC, AF.Sin, R, W)
    tt("dve", PWre, MAG, RC, ALU.mult, R, W)
    tt("dve", PWim, MAG, RS, ALU.mult, R, W)
    ts("dve", t16a, PWre[:, 1, :], -1.0, None, ALU.add, None, R, W)
    tt("dve", t16b, lr, lr, ALU.mult, R, W)
    tt("dve", t16c, aim, aim, ALU.mult, R, W)
    tt("dve", t16b, t16b, t16c, ALU.add, R, W)
    S.op("dve", lambda e: e.reciprocal(t16b, t16b), R, W)
    tt("dve", fre, t16a, lr, ALU.mult, R, W)
    tt("dve", t16c, PWim[:, 1, :], aim, ALU.mult, R, W)
    tt("dve", fre, fre, t16c, ALU.add, R, W)
    tt("dve", fre, fre, t16b, ALU.mult, R, W)
    tt("dve", fim, PWim[:, 1, :], lr, ALU.mult, R, W)
    tt("dve", t16c, t16a, aim, ALU.mult, R, W)
    tt("dve", fim, fim, t16c, ALU.subtract, R, W)
    tt("dve", fim, fim, t16b, ALU.mult, R, W)
    bj = lambda a: a.unsqueeze(2).to_broadcast([128, 16, 16])
    tt("dve", bbre, Bre, bj(fre), ALU.mult, R, W)
    tt("dve", t256, Bim, bj(fim), ALU.mult, R, W)
    tt("dve", bbre, bbre, t256, ALU.subtract, R, W)
    tt("dve", bbim, Bim, bj(fre), ALU.mult, R, W)
    tt("dve", t256, Bre, bj(fim), ALU.mult, R, W)
    tt("dve", bbim, bbim, t256, ALU.add, R, W)
    cp("dve", APWrr[:, :, :, 0], PWre[:, 4:12, :], R, [b_A8])
    cp("dve", APWrr[:, :, :, 1], PWre[:, 4:12, :], R, [b_A8])
    ts("dve", APWii[:, :, :, 0], PWim[:, 4:12, :], -1.0, None, ALU.mult, None, R, [b_A8])
    cp("dve", APWii[:, :, :, 1], PWim[:, 4:12, :], R, [b_A8])
    for (cin, cdst) in ((Cin_re, Cre), (Cin_im, Cim)):
        bk, bb_ = pbank()
        pe([(bk[:, blk * 128:(blk + 1) * 128], cin[:, blk, :], identf, True, True) for blk in range(2)],
           [b_tb, b_identf], [bb_])
        cp("dve", cdst.rearrange("p (b r) j -> p b (r j)", b=2), bk[:, 0:256].rearrange("p (b x) -> p b x", b=2),
           [bb_], W)

    def pwb(pw, n):
        return pw[:, 0:n, :].unsqueeze(3).to_broadcast([128, n, 16, 16])

    def xb(a, n):
        return a.unsqueeze(1).to_broadcast([128, n, 16, 16])
    tt("dve", ABre, xb(bbre, T5), pwb(PWre, T5), ALU.mult, R, W)
    tt("dve", tAB, xb(bbim, T5), pwb(PWim, T5), ALU.mult, R, W)
    tt("dve", ABre, ABre, tAB, ALU.subtract, R, W)
    tt("dve", ABim, xb(bbim, T5), pwb(PWre, T5), ALU.mult, R, W)
    tt("dve", tAB, xb(bbre, T5), pwb(PWim, T5), ALU.mult, R, W)
    tt("dve", ABim, ABim, tAB, ALU.add, R, W)
    tt("dve", CAre, xb(Cre, T5 + 1), pwb(PWre, T5 + 1), ALU.mult, R, W)
    tt("dve", tCA, xb(Cim, T5 + 1), pwb(PWim, T5 + 1), ALU.mult, R, W)
    tt("dve", CAre, CAre, tCA, ALU.subtract, R, W)
    tt("dve", CAimN, xb(Cre, T5 + 1), pwb(PWim, T5 + 1), ALU.mult, R, W)
    tt("dve", tCA, xb(Cim, T5 + 1), pwb(PWre, T5 + 1), ALU.mult, R, W)
    tt("dve", CAimN, CAimN, tCA, ALU.add, R, W)
    ts("dve", CAimN, CAimN, -1.0, None, ALU.mult, None, R, W)

    xi = 0
    for reim, AB in ((0, ABre), (1, ABim)):
        for tau in range(T5):
            X = Xb[xi % 2]; bX = b_Xb[xi % 2]; xi += 1
            for g2 in range(2):
                ts("dve", X[:, :, g2, :], AB[:, tau, :, :], mask2[:, g2:g2 + 1], None, ALU.mult, None,
                   [b_tb], [bX])
            bk, bb_ = pbank()
            pe([(bk[:, ft * 128:(ft + 1) * 128],
                 X[:, 4 * ft:4 * ft + 4, :, :].rearrange("p a b c -> p (a b c)"), identf, True, True)
                for ft in range(4)], [bX, b_identf], [bb_])
            for pp in range(2):
                if pp == 0:
                    S.op("act", lambda e, o=BPT[:, :, pp, reim, tau, :], i=bk.rearrange("p (f x) -> p f x", f=4),
                         m=parm[:, pp:pp + 1]: e.mul(o, i, m), [bb_, b_tb], [b_BPT])
                else:
                    ts("dve", BPT[:, :, pp, reim, tau, :], bk.rearrange("p (f x) -> p f x", f=4),
                       parm[:, pp:pp + 1], None, ALU.mult, None, [bb_, b_tb], [b_BPT])

    memset("pool", CAT.rearrange("p a b c d -> p (a b c d)"), 0.0, [b_CAT])
    for q in range(2):
        for g2 in range(2):
            for reim, CA in ((0, CAre), (1, CAimN)):
                o = CAT[:, q::2, :, reim, q * 32 + g2 * 16:q * 32 + g2 * 16 + 16]
                i = CA[:, 1:T5 + 1, q::2, :].rearrange("p s r j -> p r s j")
                ts("dve", o, i, mask2[:, g2:g2 + 1], None, ALU.mult, None, [b_tb], [b_CAT])

    memset("pool", bbpad_re.rearrange("p a b -> p (a b)"), 0.0, [b_bbpad])
    memset("pool", bbpad_im.rearrange("p a b -> p (a b)"), 0.0, [b_bbpad])
    for (bp, bsrc) in ((bbpad_re, bbre), (bbpad_im, bbim)):
        for prl in range(4):
            for g2 in range(2):
                c0 = prl * 32 + g2 * 16
                cp("dve", bp[64 * g2:64 * g2 + 64, prl::4, c0:c0 + 16], bsrc[64 * g2:64 * g2 + 64, prl::4, :],
                   [b_tb], [b_bbpad])
    for ft in range(4):
        bk, bb_ = pbank()
        mms = []
        kv = bk[:, 0:16 * T5].rearrange("p (t j) -> p t j", t=T5)
        for prl in range(4):
            pr = 4 * ft + prl
            for k_, (bp, CA) in enumerate(((bbpad_re, CAre), (bbpad_im, CAimN))):
                mms.append((kv, bp[:, pr, :], CA[:, 0:T5, pr, :], (prl == 0 and k_ == 0), (prl == 3 and k_ == 1)))
        pe(mms, [b_bbpad, b_tb], [bb_])
        for m_ in range(8):
            ts("dve", KT[:, ft, :, m_ * 16:(m_ + 1) * 16], kv, rowm[:, m_:m_ + 1], None, ALU.mult, None,
               [bb_, b_tb], [b_KT])

    S.barrier()
    AR.reset(W_END)
    b_cvg = Buf("cvg"); b_cvu = Buf("cvu"); b_cvd = Buf("cvd")
    for kt in range(8):
        rs_ = slice(kt * 128, (kt + 1) * 128)
        dma("pool", wg_b[rs_, :], w_gate[rs_, :], [], [b_cvg], b_cvg)
        dma("pool", wu_b[rs_, :], w_up[rs_, :], [], [b_cvu], b_cvu)
    for kt in range(NFF):
        rs_ = slice(kt * 128, (kt + 1) * 128)
        dma("pool", wd_b[rs_, :], w_down[rs_, :], [], [b_cvd], b_cvd)

    BT = 256
    NCH = BT // T5
    hTs = [AR.alloc([8, BT], BF16) for _ in range(2)]; b_hTs = [Buf("hT0"), Buf("hT1")]
    uTs = [AR.alloc([4, BT], BF16) for _ in range(2)]; b_uTs = [Buf("uT0"), Buf("uT1")]
    mixT = AR.alloc([8, BT], BF16); b_mixT = Buf("mixT")
    HL = AR.alloc([72 * 32], F32); b_HL = Buf("HL")
    Hpb = AR.alloc([NCH, 16, 2], BF16); b_Hpb = Buf("Hpb")
    xt = [AR.alloc([D], F32) for _ in range(2)]; b_xt = [Buf("xt0"), Buf("xt1")]
    hn0 = AR.alloc([D], BF16); hn = [hn0, hn0]; b_hn0 = Buf("hn0"); b_hn = [b_hn0, b_hn0]
    st = [AR.alloc([8], F32) for _ in range(2)]; b_st = [Buf("st0"), Buf("st1")]
    st2 = [AR.alloc([8], F32) for _ in range(2)]; b_st2 = [Buf("st20"), Buf("st21")]
    qTs = [AR.alloc([4, BT], BF16, parts=64) for _ in range(2)]; b_qTs = [Buf("qT0"), Buf("qT1")]
    kTs = [AR.alloc([4, BT], BF16, parts=64) for _ in range(2)]; b_kTs = [Buf("kT0"), Buf("kT1")]
    sgs = [AR.alloc([4, BT], BF16) for _ in range(2)]; b_sgs = [Buf("sg0"), Buf("sg1")]
    gkTs = [AR.alloc([BT], F32, parts=32) for _ in range(2)]; b_gkTs = [Buf("gkT0"), Buf("gkT1")]
    for g_, bg_ in zip(gkTs, b_gkTs):
        memset("dve", g_, 1.0, [bg_])
    vb = AR.alloc([512], BF16); b_vb = Buf("vb")
    spt = AR.alloc([256], F32); b_spt = Buf("spt")
    ebt = AR.alloc([4, 128], F32, parts=64); b_ebt = Buf("ebt")
    enbt = AR.alloc([4, 128], F32, parts=64); b_enbt = Buf("enbt")
    qd = AR.alloc([4, 128], BF16, parts=64); b_qd = Buf("qd")
    ki = AR.alloc([4, 128], BF16, parts=64); b_ki = Buf("ki")
    eet = AR.alloc([256], F32); b_eet = Buf("eet")
    kend = AR.alloc([256], BF16); b_kend = Buf("kend")
    kendm = [AR.alloc([256], BF16) for _ in range(2)]; b_kendm = [Buf("kendm0"), Buf("kendm1")]
    ebl = AR.alloc([4, 16], F32, parts=64); b_ebl = Buf("ebl")
    attT = AR.alloc([4, 128], BF16); b_attT = Buf("attT")
    osq = AR.alloc([4, 128], BF16); b_osq = Buf("osq")
    rst = AR.alloc([4, 128], F32); b_rst = Buf("rst")
    S0 = [AR.alloc([4, 128], F32, parts=64) for _ in range(2)]; b_S0 = [Buf("S00"), Buf("S01")]
    S0b = [AR.alloc([4, 128], BF16, parts=64) for _ in range(2)]; b_S0b = [Buf("S0b0"), Buf("S0b1")]
    S1 = [AR.alloc([4, 128], F32, parts=64) for _ in range(2)]; b_S1 = [Buf("S10"), Buf("S11")]
    yy = AR.alloc([2, 4, BT], F32)
    y5 = yy[:, 0, :, :]; b_y5 = Buf("y5")
    y5g = yy[:, 1, :, :]; b_y5g = Buf("y5g")
    y5gb = AR.alloc([4, BT], BF16); b_y5gb = Buf("y5gb")
    tA = AR.alloc([4, BT], F32); b_tA = Buf("tA")
    tB = AR.alloc([BT], F32); b_tB = Buf("tB")
    sqb = y5gb; b_sqb = b_y5gb
    scP = AR.alloc([16, 16, 2], F32); scQ = AR.alloc([16, 16, 2], F32); b_sc = Buf("sc")
    h0in = yy.rearrange("p a b c -> p (a b c)")[0:16, :]; b_h0in = Buf("h0in")
    s5o = h0in; b_s5o = b_h0in
    Hout = AR.alloc([2, 16], F32); b_Hout = Buf("Hout")

    xt_i = [0]

    def load_norm_transpose(src_ap, ntok, dstT, b_dstT, col0, hnslot):
        i = xt_i[0] % 2
        xt_i[0] += 1
        X, bX = xt[i], b_xt[i]
        dma("sp", X[0:ntok, :], src_ap, [], [bX], bX)
        H_, bH = hn[hnslot], b_hn[hnslot]
        s_, bs = st[hnslot], b_st[hnslot]
        act(H_[0:ntok, :], X[0:ntok, :], AF.Square, [bX], [bH, bs], accum=s_[0:ntok, 0:1])
        act(s_[0:ntok, 1:2], s_[0:ntok, 0:1], AF.Ln, [bs, b_eps], [bs], bias=eps_t[0:ntok, :], scale=1.0 / D)
        act(s_[0:ntok, 2:3], s_[0:ntok, 1:2], AF.Exp, [bs], [bs], scale=-0.5)
        S.op("act", lambda e: e.mul(H_[0:ntok, :], X[0:ntok, :], s_[0:ntok, 2:3]), [bX, bs], [bH])
        for half in range(2):
            bk, bb_ = pbank()
            pe([(bk[:, j * 128:j * 128 + ntok], H_[0:ntok, (half * 4 + j) * 128:(half * 4 + j + 1) * 128],
                 identb[0:ntok, 0:ntok], True, True) for j in range(4)], [bH, b_identb], [bb_])
            o = dstT[:, half * 4:half * 4 + 4, col0:col0 + ntok]
            iv = bk.rearrange("p (j x) -> p j x", j=4)[:, :, 0:ntok]
            cp("act", o, iv, [bb_], [b_dstT])

    class Blk:
        pass

    def mk_block(idx, kind, tiles, x1row0):
        b = Blk()
        b.idx, b.kind, b.tiles, b.x1row0 = idx, kind, tiles, x1row0
        b.ntb = sum(t[1] for t in tiles)
        b.nch = b.ntb // T5
        b.need_out = kind != "meta"
        b.mty = masks["s" if kind == "sample" else "p"]
        b.uT = uTs[idx % 2]; b.b_uT = b_uTs[idx % 2]
        b.hT = hTs[idx % 2]; b.b_hT = b_hTs[idx % 2]
        b.qT = qTs[idx % 2]; b.b_qT = b_qTs[idx % 2]
        b.kT = kTs[idx % 2]; b.b_kT = b_kTs[idx % 2]
        b.sg = sgs[idx % 2]; b.b_sg = b_sgs[idx % 2]
        b.gkT = gkTs[idx % 2]; b.b_gkT = b_gkTs[idx % 2]
        if kind == "sample":
            b.G, b.L = 16, 2
        elif kind == "meta":
            b.G, b.L = 1, b.nch
        else:
            b.G, b.L = b.nch // 8, 8
        b.HLv = HL[:, 0:b.G * (b.L + 1) * 32].rearrange("p (g l r i) -> p g l r i", g=b.G, l=b.L + 1, r=16)
        return b

    def stageA(b):
        ntb = b.ntb
        hT, b_hT, qT, b_qT, kT, b_kT = b.hT, b.b_hT, b.qT, b.b_qT, b.kT, b.b_kT
        sg, b_sg, gkT, b_gkT = b.sg, b.b_sg, b.gkT, b.b_gkT
        col = 0
        for ti, (src, ntok) in enumerate(b.tiles):
            load_norm_transpose(src, ntok, hT, b_hT, col, 0)
            col += ntok
            yield

        def fm(c0, M):
            bk, bb_ = pbank()
            pe([(bk[0:M, 0:ntb], win[:, kt, c0:c0 + M], hT[:, kt, 0:ntb], kt == 0, kt == 7) for kt in range(8)],
               [b_win, b_hT], [bb_])
            return bk, bb_
        bk, bb_ = fm(1536, 16)
        cp("act", gkT[0:16, 0:ntb], bk[0:16, 0:ntb], [bb_], [b_gkT])
        yield
        for ft in range(4):
            bk, bb_ = fm(1552 + 128 * ft, 128)
            cp("act", b.uT[:, ft, 0:ntb], bk[:, 0:ntb], [bb_], [b.b_uT])
            yield
        if b.need_out:
            for h in range(4):
                bk, bb_ = fm(64 * h, 64)
                cp("act", qT[:, h, 0:ntb], bk[0:64, 0:ntb], [bb_], [b_qT])
                bk, bb_ = fm(256 + 64 * h, 64)
                cp("act", kT[:, h, 0:ntb], bk[0:64, 0:ntb], [bb_], [b_kT])
                yield
            for h in range(4):
                bk, bb_ = fm(1024 + 128 * h, 128)
                act(sg[:, h, 0:ntb], bk[:, 0:ntb], AF.Silu, [bb_], [b_sg])
                yield

    def stageB(b):
        kind, need_out, mty = b.kind, b.need_out, b.mty
        hT, b_hT, qT, b_qT, kT, b_kT = b.hT, b.b_hT, b.qT, b.b_qT, b.kT, b.b_kT
        sg, b_sg, gkT, b_gkT = b.sg, b.b_sg, b.gkT, b.b_gkT
        mb = mty["buf"]
        col = 0
        for ti, (src, ntok) in enumerate(b.tiles):
            c0 = col
            col += ntok
            if kind == "sample":
                chunks = [(8 * i, 8 * i + 8) for i in range(16)]
            elif kind == "meta":
                chunks = [(0, 16)]
            else:
                chunks = [(0, 64), (64, 128)]
            nci = len(chunks)
            bkk, bbk = pbank()
            pe([(bkk[0:ntok, 0:256], hT[:, kt, c0:c0 + ntok], win[:, kt, 256:512], kt == 0, kt == 7) for kt in range(8)],
               [b_win, b_hT], [bbk])
            bkv, bbv = pbank()
            pe([(bkv[0:ntok, :], hT[:, kt, c0:c0 + ntok], win[:, kt, 512:1024], kt == 0, kt == 7) for kt in range(8)],
               [b_win, b_hT], [bbv])
            cp("act", vb[0:ntok, :], bkv[0:ntok, :], [bbv], [b_vb])
            bkl, bbl = pbank()
            pe([(bkl[0:ntok, 0:256], gkT[0:17, c0:c0 + ntok], wgk[0:17, :], True, True)], [b_gkT, b_wgk], [bbl])
            act(spt[0:ntok, :], bkl[0:ntok, 0:256], AF.Exp, [bbl], [b_spt], scale=-1.0)
            act(spt[0:ntok, :], spt[0:ntok, :], AF.Ln, [b_spt], [b_spt], bias=1.0)
            bke, bbe = pbank()
            pe([(bke[0:ntok, 0:256], mty["rev"][0:ntok, 0:ntok], spt[0:ntok, :], True, True)], [mb, b_spt], [bbe])
            act(eet[0:ntok, :], bke[0:ntok, 0:256], AF.Exp, [bbe], [b_eet])
            tt("dve", kend[0:ntok, :], bkk[0:ntok, 0:256], eet[0:ntok, :], ALU.mult, [bbk, b_eet], [b_kend])
            yield
            bkb, bbb = pbank()
            pe([(bkb[0:64, h * 16:h * 16 + nci], spt[0:ntok, 64 * h:64 * h + 64], mty["col"][0:ntok, 0:nci], True, True)
                for h in range(4)], [mb, b_spt], [bbb])
            act(ebl[:, :, 0:nci], bkb[0:64, 0:64].rearrange("p (h c) -> p h c", h=4)[:, :, 0:nci], AF.Exp,
                [bbb], [b_ebl])
            if need_out:
                bkt, bbt = pbank()
                pe([(bkt[0:64, h * 128:h * 128 + ntok], spt[0:ntok, 64 * h:64 * h + 64], mty["cs"][0:ntok, 0:ntok],
                     True, True) for h in range(4)], [mb, b_spt], [bbt])
                bv = bkt[0:64, :].rearrange("p (h t) -> p h t", h=4)[:, :, 0:ntok]
                act(ebt[:, :, 0:ntok], bv, AF.Exp, [bbt], [b_ebt])
                act(enbt[:, :, 0:ntok], bv, AF.Exp, [bbt], [b_enbt], scale=-1.0)
                stt("dve", qd[:, :, 0:ntok], qT[:, :, c0:c0 + ntok], 0.125, ebt[:, :, 0:ntok], ALU.mult, ALU.mult,
                    [b_qT, b_ebt], [b_qd])
                yield
                tt("dve", ki[:, :, 0:ntok], kT[:, :, c0:c0 + ntok], enbt[:, :, 0:ntok], ALU.mult, [b_kT, b_enbt], [b_ki])
                yield
                bka, bba = pbank()
                pe([(bka[0:ntok, h * 128:h * 128 + ntok], ki[:, h, 0:ntok], qd[:, h, 0:ntok], True, True)
                    for h in range(4)], [b_ki, b_qd], [bba])
                tt("dve", attT[0:ntok, :, 0:ntok], bka[0:ntok, :].rearrange("p (h t) -> p h t", h=4)[:, :, 0:ntok],
                   mty["att"][0:ntok, 0:ntok].unsqueeze(1).to_broadcast([ntok, 4, ntok]), ALU.mult,
                   [bba, mb], [b_attT])
                yield
                bko, bbo = banks[7][:, :], bank_bufs[7]
                pe([(bko[:, h * 128:h * 128 + ntok], vb[0:ntok, 128 * h:128 * h + 128], attT[0:ntok, h, 0:ntok],
                     h == 0, False) for h in range(4)], [b_vb, b_attT], [bbo], skip=True)

            if kind != "sample":
                for ci, (a, b_) in enumerate(chunks):
                    if need_out:
                        pe([(bko[:, h * 128 + a:h * 128 + b_], Sgb[:, h, :], qd[:, h, a:b_], False, ci == nci - 1)
                            for h in range(4)], [b_Sgb, b_qd], [bbo], skip=True)
                    km = kendm[ci % 2]; bkm = b_kendm[ci % 2]
                    S.op("act", lambda e, o=km[0:ntok, :], i_=kend[0:ntok, :], m_=mty["col1"][0:ntok, ci:ci + 1]: e.mul(o, i_, m_),
                         [b_kend, mb], [bkm])
                    bks, bbs = pbank()
                    pe([(bks[0:64, h * 128:(h + 1) * 128], km[0:ntok, 64 * h:64 * h + 64], vb[0:ntok, 128 * h:128 * h + 128],
                         True, True) for h in range(4)], [bkm, b_vb], [bbs])
                    for h in range(4):
                        stt("dve", Sg[:, h, :], Sg[:, h, :], ebl[:, h, ci:ci + 1], bks[0:64, h * 128:(h + 1) * 128],
                            ALU.mult, ALU.add, [b_Sg, b_ebl, bbs], [b_Sg])
                        yield
                    cp("act", Sgb.rearrange("p a b -> p (a b)"), Sg.rearrange("p a b -> p (a b)"), [b_Sg], [b_Sgb])
            else:
                for i in range(16):
                    a, b_ = chunks[i]
                    s0, bs0 = S0[i % 2], b_S0[i % 2]
                    s0b, bs0b = S0b[i % 2], b_S0b[i % 2]
                    s1, bs1 = S1[i % 2], b_S1[i % 2]
                    dma("sp", s0, sgla[i].rearrange("h d v -> d h v"), [], [bs0], bs0)
                    cp("act", s0b.rearrange("p a b -> p (a b)"), s0.rearrange("p a b -> p (a b)"), [bs0], [bs0b])
                    pe([(bko[:, h * 128 + a:h * 128 + b_], s0b[:, h, :], qd[:, h, a:b_], False, i == 15)
                        for h in range(4)], [bs0b, b_qd], [bbo], skip=True)
                    km = kendm[i % 2]; bkm = b_kendm[i % 2]
                    S.op("act", lambda e, o=km[0:ntok, :], i_=kend[0:ntok, :], m_=mty["col1"][0:ntok, i:i + 1]: e.mul(o, i_, m_),
                         [b_kend, mb], [bkm])
                    bks, bbs = pbank()
                    pe([(bks[0:64, h * 128:(h + 1) * 128], km[0:ntok, 64 * h:64 * h + 64], vb[0:ntok, 128 * h:128 * h + 128],
                         True, True) for h in range(4)], [bkm, b_vb], [bbs])
                    for h in range(4):
                        stt("dve", s1[:, h, :], s0[:, h, :], ebl[:, h, i:i + 1], bks[0:64, h * 128:(h + 1) * 128],
                            ALU.mult, ALU.add, [bs0, b_ebl, bbs], [bs1])
                        yield
                    dma("sp", glas[i].rearrange("h d v -> d h v"), s1, [bs1], [], bs1)

            if need_out:
                ov = bko.rearrange("p (h t) -> p h t", h=4)[:, :, 0:ntok]
                act(osq[:, :, 0:ntok], ov, AF.Square, [bbo], [b_osq])
                bkq, bbq = pbank()
                pe([(bkq[:, h * 128:h * 128 + ntok], onesb, osq[:, h, 0:ntok], True, True) for h in range(4)],
                   [b_onesb, b_osq], [bbq])
                qv = bkq.rearrange("p (h t) -> p h t", h=4)[:, :, 0:ntok]
                act(rst[:, :, 0:ntok], qv, AF.Sqrt, [bbq, b_eps], [b_rst], bias=eps_t, scale=1.0 / 128)
                S.op("dve", lambda e, o=rst[:, :, 0:ntok]: e.reciprocal(o, o), [b_rst], [b_rst])
                yield
                tt("dve", rst[:, :, 0:ntok], ov, rst[:, :, 0:ntok], ALU.mult, [bbo, b_rst], [b_rst])
                yield
                stt("dve", mixT[:, 0:4, c0:c0 + ntok], rst[:, :, 0:ntok], glan[:, 0:1], sg[:, :, c0:c0 + ntok],
                    ALU.mult, ALU.mult, [b_rst, b_vec, b_sg], [b_mixT])
                yield

    def stageC(b):
        ntb, nch, G, L = b.ntb, b.nch, b.G, b.L
        for bq in range(4):
            hf, fth = bq // 2, bq % 2
            bk, bb_ = pbank()
            mms = []
            for slot in range(8):
                ft = 2 * fth + slot // 4
                q = (slot // 2) % 2
                reim = slot % 2
                for tau in range(T5):
                    s_ = T5 - 1 - tau
                    mms.append((bk[:, slot * 64:slot * 64 + nch],
                                BPT[64 * hf:64 * hf + 64, ft, q, reim, tau, :],
                                b.uT[64 * hf:64 * hf + 64, ft, s_:ntb:T5], tau == 0, tau == T5 - 1))
            pe(mms, [b_BPT, b.b_uT], [bb_])
            for bsel in range(2):
                src = bk[:, bsel * 256:(bsel + 1) * 256].rearrange("p (x c) -> p x c", x=4)[:, :, 0:nch]
                src = src.rearrange("p x (g l) -> p x g l", g=G)
                pr0 = 4 * (2 * fth + bsel) + 2 * hf
                o = b.HLv[:, :, 1:L + 1, pr0:pr0 + 2, :].rearrange("p g l r i -> p (r i) g l")
                cp("act" if bsel else "dve", o, src, [bb_], [b_HL])

    def cmul_add(eng, dst, Hc, arr, aii, shp4, rb, wb, accumulate_into=None):
        X = shp4[1]
        P_ = scP[:, 0:X, :, :]
        Q_ = scQ[:, 0:X, :, :]
        tt(eng, P_, Hc, arr, ALU.mult, rb + [b_A8], [b_sc])
        yield
        tt(eng, Q_[:, :, :, 0], Hc[:, :, :, 1], aii[:, :, :, 0], ALU.mult, rb + [b_A8], [b_sc])
        yield
        tt(eng, Q_[:, :, :, 1], Hc[:, :, :, 0], aii[:, :, :, 1], ALU.mult, rb + [b_A8], [b_sc])
        yield
        tt(eng, P_, P_, Q_, ALU.add, [b_sc], [b_sc])
        yield
        tt(eng, dst, P_, accumulate_into, ALU.add, [b_sc] + rb, wb)
        yield

    def stageD(b):
        G, L, HLv = b.G, b.L, b.HLv
        eng = "dve"
        if b.kind == "sample":
            for reim, srcd in ((0, s5re0), (1, s5im0)):
                bk, bb_ = pbank()
                dma("sp", h0in, srcd, [], [b_h0in, b_y5, b_y5g], b_h0in)
                pe([(bk[:, pr * 16:pr * 16 + 16], h0in[:, pr * 128:(pr + 1) * 128], identf[0:16, 0:16], True, True)
                    for pr in range(16)], [b_h0in, b_y5, b_y5g, b_identf], [bb_])
                cp("dve", HLv[:, :, 0, :, reim].rearrange("p c r -> p r c"),
                   bk[:, 0:256].rearrange("p (r c) -> p r c", r=16), [bb_], [b_HL])
        else:
            memset("dve", HLv[:, :, 0, :, :], 0.0, [b_HL])
        shp = [128, G, 16, 2]
        arr = APWrr[:, 0, :, :].unsqueeze(1).to_broadcast(shp)
        aii = APWii[:, 0, :, :].unsqueeze(1).to_broadcast(shp)
        for l in range(L):
            yield from cmul_add(eng, HLv[:, :, l + 1, :, :], HLv[:, :, l, :, :], arr, aii, shp, [b_HL], [b_HL],
                                accumulate_into=HLv[:, :, l + 1, :, :])
        if b.kind != "sample":
            shpL = [128, L, 16, 2]
            for g in range(G):
                Hin = Hcar if g == 0 else HLv[:, g - 1, L, :, :]
                rb = [b_Hcar] if g == 0 else [b_HL]
                Hb = Hin.unsqueeze(1).to_broadcast(shpL)
                yield from cmul_add(eng, HLv[:, g, 1:L + 1, :, :], Hb, APWrr[:, 0:L, :, :], APWii[:, 0:L, :, :], shpL,
                                    rb + [b_HL], [b_HL], accumulate_into=HLv[:, g, 1:L + 1, :, :])
                cp("act", HLv[:, g, 0, :, :], Hin, rb, [b_HL])
            cp("act", Hcar, HLv[:, G - 1, L, :, :], [b_HL], [b_Hcar])
        if b.need_out:
            cp("act", Hpb[:, 0:b.nch, :, :].rearrange("p (g l) r i -> p g l (r i)", g=G),
               HLv[:, :, 0:L, :, :].rearrange("p g l r i -> p g l (r i)"), [b_HL], [b_Hpb])
        if b.kind == "sample":
            for reim, dstd in ((0, s5res), (1, s5ims)):
                for q4 in range(4):
                    bk, bb_ = pbank()
                    pe([(bk[0:16, j * 128:(j + 1) * 128], HLv[:, :, L, 4 * q4 + j, reim], identf, True, True)
                        for j in range(4)], [b_HL, b_identf], [bb_])
                    cp("act", s5o[:, q4 * 512:(q4 + 1) * 512], bk[0:16, :], [bb_], [b_s5o, b_y5, b_y5g])
                dma("sp", dstd, s5o, [b_s5o, b_y5, b_y5g], [], b_s5o)

    def stageE(b):
        ntb, nch = b.ntb, b.nch
        uT, b_uT = b.uT, b.b_uT
        for ft in range(4):
            bk, bb_ = pbank()
            mms = []
            for s_ in range(T5):
                for tau in range(s_ + 1):
                    mms.append((bk[:, s_:ntb:T5], KT[:, ft, tau, :], uT[:, ft, (s_ - tau):ntb:T5],
                                len(mms) == 0, False))
                for prl in range(4):
                    pr = 4 * ft + prl
                    hf = prl // 2
                    for reim in range(2):
                        mms.append((bk[64 * hf:64 * hf + 64, s_:ntb:T5], CAT[:, pr, s_, reim, :],
                                    Hpb[:, 0:nch, pr, reim], False, False))
            mms[-1] = mms[-1][:4] + (True,)
            pe(mms, [b_KT, b_uT, b_CAT, b_Hpb], [bb_], skip=True)
            stt("dve", y5[:, ft, 0:ntb], uT[:, ft, 0:ntb], dsk[:, ft:ft + 1], bk[:, 0:ntb], ALU.mult, ALU.add,
                [b_uT, b_vec, bb_], [b_y5])
        yv = y5[:, :, 0:ntb]; gv = y5g[:, :, 0:ntb]; av = tA[:, :, 0:ntb]
        act(av, yv, AF.Square, [b_y5], [b_tA])
        ts("dve", av, av, 0.044715, 1.0, ALU.mult, ALU.add, [b_tA], [b_tA])
        tt("dve", av, av, yv, ALU.mult, [b_tA, b_y5], [b_tA])
        act(av, av, AF.Sigmoid, [b_tA], [b_tA], scale=2.0 * math.sqrt(2.0 / math.pi))
        tt("dve", gv, yv, av, ALU.mult, [b_tA, b_y5], [b_y5g])
        cp("act", y5gb[:, :, 0:ntb], gv, [b_y5g], [b_y5gb])
        for fo in range(4):
            bk, bb_ = pbank()
            pe([(bk[:, 0:ntb], wglu[:, fi, 128 * fo:128 * fo + 128], y5gb[:, fi, 0:ntb], fi == 0, fi == 3)
                for fi in range(4)], [b_wglu, b_y5gb], [bb_])
            act(tA[:, fo, 0:ntb], bk[:, 0:ntb], AF.Sigmoid, [bb_], [b_tA])
        tt("dve", gv, gv, av, ALU.mult, [b_tA, b_y5g], [b_y5g])
        act(sqb[:, :, 0:ntb], gv, AF.Square, [b_y5g], [b_sqb])
        bk, bb_ = pbank()
        pe([(bk[:, 0:ntb], onesb, sqb[:, fi, 0:ntb], fi == 0, fi == 3) for fi in range(4)], [b_onesb, b_sqb], [bb_])
        act(tB[:, 0:ntb], bk[:, 0:ntb], AF.Sqrt, [bb_, b_eps], [b_tB], bias=eps_t, scale=1.0 / 512)
        S.op("dve", lambda e, o=tB[:, 0:ntb]: e.reciprocal(o, o), [b_tB], [b_tB])
        for fi in range(4):
            stt("dve", mixT[:, 4 + fi, 0:ntb], gv[:, fi, :], s5n[:, fi:fi + 1], tB[:, 0:ntb],
                ALU.mult, ALU.mult, [b_y5g, b_vec, b_tB], [b_mixT])

    def stageF(b):
        col = 0
        tAf = tA.rearrange("p a b -> p (a b)")
        junk = y5gb.rearrange("p a b -> p (a b)")
        for ti, (src, ntok) in enumerate(b.tiles):
            c0 = col
            col += ntok
            i = xt_i[0] % 2
            xt_i[0] += 1
            X, bX = xt[i], b_xt[i]
            dma("sp", X[0:ntok, :], src, [], [bX], bX)
            s_, bs = st2[ti % 2], b_st2[ti % 2]
            hbk = []
            for half in range(2):
                bk, bb_ = pbank()
                pe([(bk[0:ntok, :], mixT[:, k, c0:c0 + ntok], wo[:, k, half * 512:(half + 1) * 512], k == 0, k == 7)
                    for k in range(8)], [b_mixT, b_wo], [bb_])
                act(junk[0:ntok, half * 512:(half + 1) * 512], bk[0:ntok, :], AF.Square, [bb_], [b_y5gb, bs],
                    accum=s_[0:ntok, half:half + 1])
                hbk.append((bk, bb_))
            tt("dve", s_[0:ntok, 2:3], s_[0:ntok, 0:1], s_[0:ntok, 1:2], ALU.add, [bs], [bs])
            act(s_[0:ntok, 3:4], s_[0:ntok, 2:3], AF.Sqrt, [bs, b_eps], [bs], bias=eps_t[0:ntok, :], scale=1.0 / D)
            S.op("dve", lambda e, o=s_[0:ntok, 4:5], i_=s_[0:ntok, 3:4]: e.reciprocal(o, i_), [bs], [bs])
            for half in range(2):
                bk, bb_ = hbk[half]
                sl_ = slice(half * 512, (half + 1) * 512)
                stt("dve", tAf[0:ntok, sl_], bk[0:ntok, :], s_[0:ntok, 4:5], gpm_row[0:ntok, sl_], ALU.mult, ALU.mult,
                    [bb_, bs, b_gpm_row], [b_tA])
            tt("dve", X[0:ntok, :], X[0:ntok, :], tAf[0:ntok, :], ALU.add, [b_tA, bX], [bX])
            dma("sp", x1d[b.x1row0 + c0:b.x1row0 + c0 + ntok, :], X[0:ntok, :], [bX], [], bX)

    blocks = [mk_block(0, "meta", [(meta, NMETA)], 0)]
    blocks.append(mk_block(len(blocks), "sample", [(xs, 128)], SEQ))
    for blk in range(SEQ // BT):
        tiles = [(xp[blk * BT + j * 128: blk * BT + (j + 1) * 128, :], 128) for j in range(BT // 128)]
        blocks.append(mk_block(len(blocks), "prompt", tiles, blk * BT))
    nb = len(blocks)
    def run_all(g):
        for _ in g:
            pass

    def merge(gens_weights):
        live = [[g, w] for (g, w) in gens_weights]
        while live:
            for item in list(live):
                g, w = item
                for _ in range(w):
                    try:
                        next(g)
                    except StopIteration:
                        live.remove(item)
                        break

    run_all(stageA(blocks[0]))
    for i, b in enumerate(blocks):
        stageC(b)
        if b.kind == "meta":
            run_all(stageB(b))
            run_all(stageD(b))
            if i + 1 < nb:
                run_all(stageA(blocks[i + 1]))
        else:
            gl = [(stageB(b), 1), (stageD(b), 3)]
            if i + 1 < nb:
                gl.append((stageA(blocks[i + 1]), 1))
            merge(gl)
        if b.need_out:
            stageE(b)
            stageF(b)
        if i == nb - 1:
            dma("sp", glap.rearrange("h d v -> d h v"), Sg, [b_Sg], [], b_Sg)
            cp("dve", Hout.rearrange("p i r -> p r i"), Hcar, [b_Hcar], [b_Hout])
            bkh, bbh = pbank()
            pe([(bkh[0:16, 128 * i_:128 * i_ + 128], Hout[:, i_, :], identf, True, True) for i_ in range(2)],
               [b_Hout, b_identf], [bbh])
            cp("dve", tB[0:16, 0:256], bkh[0:16, 0:256], [bbh], [b_tB])
            dma("sp", s5rep, tB[0:16, 0:128], [b_tB], [], b_tB)
            dma("sp", s5imp, tB[0:16, 128:256], [b_tB], [], b_tB)
    S.barrier()

    AR.reset(PERSIST_MARK)
    gpf_row = AR.alloc([D], F32); b_gpf_row = Buf("gpf_row")
    dma("sp", gpf_row, g_post_ffn.partition_broadcast(128), [], [b_gpf_row], b_gpf_row)
    wg = AR.alloc([8, DFF], BF16)
    wu = AR.alloc([8, DFF], BF16)
    wd = AR.alloc([NFF, D], BF16)
    GT = 512
    h2T = AR.alloc([8, GT], BF16); b_h2T = Buf("h2T")
    actT = AR.alloc([NFF, GT], BF16); b_actT = Buf("actT")
    x1r = [AR.alloc([D], F32) for _ in range(2)]; b_x1r = [Buf("x1r0"), Buf("x1r1")]
    sgt = [AR.alloc([GT], F32) for _ in range(2)]; b_sgt = [Buf("sgt0"), Buf("sgt1")]
    otmp = AR.alloc([D], F32); b_otmp = Buf("otmp")
    fhn = [AR.alloc([D], BF16) for _ in range(4)]; b_fhn = [Buf("fhn%d" % i) for i in range(4)]
    fst = [AR.alloc([8], F32) for _ in range(4)]; b_fst = [Buf("fst%d" % i) for i in range(4)]
    fst2 = [AR.alloc([8], F32) for _ in range(2)]; b_fst2 = [Buf("fst20"), Buf("fst21")]

    b_wgh = [Buf("wgh0"), Buf("wgh1")]
    b_wuh = [Buf("wuh0"), Buf("wuh1")]
    b_wdh = Buf("wdh")
    def issue_ffn_weight_loads():
        wgv = wg_b.rearrange("(kt p) c -> p kt c", p=128)
        wuv = wu_b.rearrange("(kt p) c -> p kt c", p=128)
        wdv = wd_b.rearrange("(kt p) c -> p kt c", p=128)
        for hh in range(2):
            cs_ = slice(hh * 1408, (hh + 1) * 1408)
            dma("sp", wg[:, :, cs_], wgv[:, :, cs_], [b_cvg], [b_wgh[hh]], b_wgh[hh])
            dma("act", wu[:, :, cs_], wuv[:, :, cs_], [b_cvu], [b_wuh[hh]], b_wuh[hh])
        dma("sp", wd[:, 0:11, :], wdv[:, 0:11, :], [b_cvd], [b_wdh], b_wdh)
        dma("act", wd[:, 11:22, :], wdv[:, 11:22, :], [b_cvd], [b_wdh], b_wdh)

    groups = [(g * GT, GT, yp, g * GT) for g in range(SEQ // GT)] + [(SEQ, 128, ys, 0)]
    xi = [0]

    def prepA(grp):
        (row0, ng, ydst, yrow0) = grp
        for j in range(ng // 128):
            X1, bX1 = x1r[xi[0] % 2], b_x1r[xi[0] % 2]
            xi[0] += 1
            H_, bH = fhn[j], b_fhn[j]
            s_, bs = fst[j], b_fst[j]
            dma("sp", X1, x1d[row0 + j * 128:row0 + (j + 1) * 128, :], [], [bX1], bX1)
            act(H_, X1, AF.Square, [bX1], [bH, bs], accum=s_[:, 0:1])
            act(s_[:, 1:2], s_[:, 0:1], AF.Ln, [bs, b_eps], [bs], bias=eps_t, scale=1.0 / D)
            act(s_[:, 2:3], s_[:, 1:2], AF.Exp, [bs], [bs], scale=-0.5)
            S.op("act", lambda e, o=H_, i_=X1, m_=s_[:, 2:3]: e.mul(o, i_, m_), [bX1, bs], [bH])

    def prepB(grp):
        (row0, ng, ydst, yrow0) = grp
        for j in range(ng // 128):
            H_, bH = fhn[j], b_fhn[j]
            for half in range(2):
                bk, bb_ = pbank()
                pe([(bk[:, q * 128:(q + 1) * 128], H_[:, (half * 4 + q) * 128:(half * 4 + q + 1) * 128], identb, True, True)
                    for q in range(4)], [bH, b_identb], [bb_])
                for q in range(4):
                    kt_ = half * 4 + q
                    if q % 2 == 0:
                        S.op("act", lambda e, o=h2T[:, kt_, j * 128:(j + 1) * 128], i_=bk[:, q * 128:(q + 1) * 128],
                             m_=gpf_col[:, kt_:kt_ + 1]: e.mul(o, i_, m_), [bb_, b_gcol], [b_h2T])
                    else:
                        ts("dve", h2T[:, kt_, j * 128:(j + 1) * 128], bk[:, q * 128:(q + 1) * 128],
                           gpf_col[:, kt_:kt_ + 1], None, ALU.mult, None, [bb_, b_gcol], [b_h2T])

    def gate_up(grp):
        (row0, ng, ydst, yrow0) = grp
        for ff in range(NFF):
            bkg, bbg = pbank()
            pe([(bkg[:, 0:ng], wg[:, kt, ff * 128:(ff + 1) * 128], h2T[:, kt, 0:ng], kt == 0, kt == 7) for kt in range(8)],
               [b_wgh[ff // 11], b_h2T], [bbg])
            bku, bbu = pbank()
            pe([(bku[:, 0:ng], wu[:, kt, ff * 128:(ff + 1) * 128], h2T[:, kt, 0:ng], kt == 0, kt == 7) for kt in range(8)],
               [b_wuh[ff // 11], b_h2T], [bbu])
            sgl, bsgl = sgt[ff % 2], b_sgt[ff % 2]
            act(sgl[:, 0:ng], bkg[:, 0:ng], AF.Silu, [bbg], [bsgl])
            tt("dve", actT[:, ff, 0:ng], sgl[:, 0:ng], bku[:, 0:ng], ALU.mult, [bsgl, bbu], [b_actT])

    def down_tail(grp):
        (row0, ng, ydst, yrow0) = grp
        for j in range(ng // 128):
            s_, bs = fst2[j % 2], b_fst2[j % 2]
            X1, bX1 = x1r[xi[0] % 2], b_x1r[xi[0] % 2]
            xi[0] += 1
            dma("sp", X1, x1d[row0 + j * 128:row0 + (j + 1) * 128, :], [], [bX1], bX1)
            hbk = []
            for half in range(2):
                bk, bb_ = pbank()
                pe([(bk, actT[:, ff, j * 128:(j + 1) * 128], wd[:, ff, half * 512:(half + 1) * 512], ff == 0, ff == NFF - 1)
                    for ff in range(NFF)], [b_actT, b_wdh], [bb_])
                act(sgt[half][:, 0:512], bk, AF.Square, [bb_], [b_sgt[half], bs], accum=s_[:, half:half + 1])
                hbk.append((bk, bb_))
            tt("dve", s_[:, 2:3], s_[:, 0:1], s_[:, 1:2], ALU.add, [bs], [bs])
            act(s_[:, 3:4], s_[:, 2:3], AF.Sqrt, [bs, b_eps], [bs], bias=eps_t, scale=1.0 / D)
            S.op("dve", lambda e, o=s_[:, 4:5], i_=s_[:, 3:4]: e.reciprocal(o, i_), [bs], [bs])
            for half in range(2):
                bk, bb_ = hbk[half]
                sl_ = slice(half * 512, (half + 1) * 512)
                stt("dve", otmp[:, sl_], bk, s_[:, 4:5], gpf_row[:, sl_], ALU.mult, ALU.mult, [bb_, bs, b_gpf_row], [b_otmp])
            tt("dve", X1, X1, otmp, ALU.add, [b_otmp, bX1], [bX1])
            dma("sp", ydst[yrow0 + j * 128:yrow0 + (j + 1) * 128, :], X1, [bX1], [], bX1)

    prepA(groups[0])
    prepB(groups[0])
    issue_ffn_weight_loads()
    for gi, grp in enumerate(groups):
        gate_up(grp)
        if gi + 1 < len(groups):
            prepA(groups[gi + 1])
        down_tail(grp)
        if gi + 1 < len(groups):
            prepB(groups[gi + 1])

    S.barrier()
    S.emit()
    es.close()
    return nc


_NC_CACHE = {}


def kernel(**inp):
    f = lambda a: np.ascontiguousarray(np.asarray(a, dtype=np.float32))
    if "nc" not in _NC_CACHE:
        _NC_CACHE["nc"] = build_nc()
    nc = _NC_CACHE["nc"]
    shared = {
        "meta": f(inp["meta_tokens"]),
        "g_pre_mix": f(inp["g_pre_mix"][0]), "w_in": f(inp["w_in"][0]), "w_gk2": f(inp["w_gk2"][0]),
        "b_gk": f(inp["b_gk"][0]).reshape(1, 256), "gla_norm": f(inp["gla_norm"][0]).reshape(128, 1),
        "s5_a_re": f(inp["s5_a_re"][0]).reshape(2048), "s5_a_im": f(inp["s5_a_im"][0]).reshape(2048),
        "s5_b_re": f(inp["s5_b_re"][0]).reshape(2048, 16), "s5_b_im": f(inp["s5_b_im"][0]).reshape(2048, 16),
        "s5_c_re": f(inp["s5_c_re"][0]), "s5_c_im": f(inp["s5_c_im"][0]),
        "s5_d": f(inp["s5_d"][0]), "s5_log_dt": f(inp["s5_log_dt"][0]).reshape(1, 32),
        "w_s5_glu": f(inp["w_s5_glu"][0]), "s5_norm": f(inp["s5_norm"][0]),
        "w_o": f(inp["w_o"][0]), "g_post_mix": f(inp["g_post_mix"][0]).reshape(1, D),
        "g_pre_ffn": f(inp["g_pre_ffn"][0]),
        "w_gate": f(inp["w_gate"][0]), "w_up": f(inp["w_up"][0]), "w_down": f(inp["w_down"][0]),
        "g_post_ffn": f(inp["g_post_ffn"][0]).reshape(1, D),
    }
    in_maps = []
    for c in range(NCORES):
        m = dict(shared)
        m["xp"] = f(inp["x_prompt"][c])
        m["xs"] = f(inp["x_sample"][16 * c:16 * c + 16]).reshape(128, D)
        m["sgla"] = f(inp["state_gla"][0, 16 * c:16 * c + 16])
        m["s5re0"] = f(inp["state_s5_re"][0, 16 * c:16 * c + 16]).reshape(16, 2048)
        m["s5im0"] = f(inp["state_s5_im"][0, 16 * c:16 * c + 16]).reshape(16, 2048)
        in_maps.append(m)
    res = run_bass_kernel_spmd(nc, in_maps, core_ids=list(range(NCORES)))
    r = res.results
    y_prompt = np.stack([r[c]["yp"] for c in range(NCORES)]).astype(np.float32)
    y_sample = np.concatenate([r[c]["ys"].reshape(16, 8, D) for c in range(NCORES)]).astype(np.float32)
    gla_p = np.stack([r[c]["glap"] for c in range(NCORES)])[None].astype(np.float32)
    s5re_p = np.stack([r[c]["s5rep"].reshape(32, 64) for c in range(NCORES)])[None].astype(np.float32)
    s5im_p = np.stack([r[c]["s5imp"].reshape(32, 64) for c in range(NCORES)])[None].astype(np.float32)
    gla_s = np.concatenate([r[c]["glas"] for c in range(NCORES)])[None].astype(np.float32)
    s5re_s = np.concatenate([r[c]["s5res"].reshape(16, 32, 64) for c in range(NCORES)])[None].astype(np.float32)
    s5im_s = np.concatenate([r[c]["s5ims"].reshape(16, 32, 64) for c in range(NCORES)])[None].astype(np.float32)
    return (y_prompt, y_sample, gla_p, s5re_p, s5im_p, gla_s, s5re_s, s5im_s)
```

```python
import contextlib
import math
import os
import numpy as np
import concourse.bass as bass
import concourse.mybir as mybir
from concourse.bass_utils import run_bass_kernel_spmd

F32 = mybir.dt.float32
BF16 = mybir.dt.bfloat16
I32 = mybir.dt.int32
AF = mybir.ActivationFunctionType
ALU = mybir.AluOpType

D = 1024
DIN = 2064
DFF = 2816
NFF = 22
SEQ = 2048
NMETA = 16
T5 = 4
EPS = 1e-6
TWO_PI = 2.0 * math.pi
TWO_PI_HI = float(np.float32(TWO_PI))
TWO_PI_LO = TWO_PI - TWO_PI_HI
NCORES = 8


class Buf:
    def __init__(self, name):
        self.name = name
        self.last_w = None
        self.readers = []
        self.chan = None


class Sched:
    ENGS = ["pe", "act", "dve", "pool", "sp"]

    def __init__(self, nc):
        self.nc = nc
        self.ops = {e: [] for e in self.ENGS}
        self.count = {e: 0 for e in self.ENGS}
        self.seen = {e: {} for e in self.ENGS}
        self.nchan = 0
        self.chan_count = {}

    def _deps(self, eng, reads, writes):
        deps = []
        for b in reads:
            if b.last_w is not None:
                deps.append(b.last_w)
        for b in writes:
            if b.last_w is not None:
                deps.append(b.last_w)
            deps.extend(b.readers)
        seen = self.seen[eng]
        d = {}
        for (k, v) in deps:
            if k == "pe" and eng == "pe":
                continue
            if seen.get(k, 0) >= v:
                continue
            d[k] = max(d.get(k, 0), v)
        for k, v in d.items():
            seen[k] = v
        return list(d.items())

    def _commit(self, ev, reads, writes):
        for b in writes:
            b.last_w = ev
            b.readers = []
        for b in reads:
            if b not in writes:
                b.readers.append(ev)
                if len(b.readers) > 64:
                    dd = {}
                    for (k, v) in b.readers:
                        dd[k] = max(dd.get(k, 0), v)
                    b.readers = list(dd.items())

    def op(self, eng, fn, reads=(), writes=(), selfwait=False):
        waits = self._deps(eng, reads, writes)
        if selfwait and self.count[eng] > 0:
            waits = [w for w in waits if w[0] != eng] + [(eng, self.count[eng])]
        self.count[eng] += 1
        ev = (eng, self.count[eng])
        self.ops[eng].append((waits, fn, (eng, 1)))
        self._commit(ev, reads, writes)
        return ev

    def dma(self, eng, fn, reads=(), writes=(), chan=None):
        waits = self._deps(eng, reads, writes)
        if chan.chan is None:
            chan.chan = "dma%d" % self.nchan
            self.nchan += 1
            self.chan_count[chan.chan] = 0
        self.chan_count[chan.chan] += 16
        ev = (chan.chan, self.chan_count[chan.chan])
        self.ops[eng].append((waits, fn, (chan.chan, 16)))
        self._commit(ev, reads, writes)
        return ev

    def barrier(self):
        evs = [(e, self.count[e]) for e in self.ENGS if self.count[e] > 0]
        evs += [(k, v) for k, v in self.chan_count.items() if v > 0]
        for e in self.ENGS:
            waits = []
            for (k, v) in evs:
                if k == e:
                    continue
                if self.seen[e].get(k, 0) >= v:
                    continue
                self.seen[e][k] = v
                waits.append((k, v))
            if waits:
                self.ops[e].append((waits, None, None))

    def emit(self):
        nc = self.nc
        with contextlib.ExitStack() as es:
            sems = {}
            for e in self.ENGS:
                sems[e] = es.enter_context(nc.semaphore("s_" + e))
            for k in self.chan_count:
                sems[k] = es.enter_context(nc.semaphore("s_" + k))
            block = es.enter_context(nc.Block())
            engmap = {"pe": block.tensor, "act": block.scalar, "dve": block.vector,
                      "pool": block.gpsimd, "sp": block.sync}

            def mk(ename):
                def body(eng):
                    for (waits, fn, inc) in self.ops[ename]:
                        for (k, v) in waits:
                            eng.wait_ge(sems[k], v)
                        if fn is not None:
                            ins = fn(eng)
                            ins.then_inc(sems[inc[0]], inc[1])
                return body
            for e in self.ENGS:
                if self.ops[e]:
                    engmap[e](mk(e))


def _resh(v, shape):
    if len(shape) == 1:
        return v
    names = ["d%d" % i for i in range(len(shape))]
    s = "p (" + " ".join(names) + ") -> p " + " ".join(names)
    kw = {names[i]: int(shape[i]) for i in range(len(shape))}
    return v.rearrange(s, **kw)


class Arena:
    def __init__(self, ap, nwords):
        self.ap = ap
        self.n = nwords
        self.off = 0
        self.hi = 0

    def mark(self):
        return self.off

    def reset(self, m):
        self.off = m

    def alloc(self, shape, dtype, parts=128, name=None):
        esz = {F32: 4, BF16: 2, I32: 4}[dtype]
        nelem = int(np.prod(shape))
        nw = (nelem * esz + 3) // 4
        nw = (nw + 7) // 8 * 8
        o = self.off
        self.off += nw
        self.hi = max(self.hi, self.off)
        assert self.off <= self.n, "arena overflow %s %d > %d" % (name, self.off * 4, self.n * 4)
        v = self.ap[0:parts, o:o + (nelem * esz + 3) // 4]
        if dtype != F32:
            v = v.bitcast(dtype)
            v = v[:, 0:nelem]
        return _resh(v, list(shape))


def build_nc():
    nc = bass.Bass("TRN2", target_bir_lowering=False)

    def din(name, shape):
        return nc.dram_tensor(name, list(shape), F32, kind="ExternalInput").ap()

    def dout(name, shape):
        return nc.dram_tensor(name, list(shape), F32, kind="ExternalOutput").ap()

    xp = din("xp", [SEQ, D]); meta = din("meta", [NMETA, D]); xs = din("xs", [128, D])
    sgla = din("sgla", [16, 4, 64, 128]); s5re0 = din("s5re0", [16, 2048]); s5im0 = din("s5im0", [16, 2048])
    g_pre_mix = din("g_pre_mix", [D]); w_in = din("w_in", [D, DIN]); w_gk2 = din("w_gk2", [16, 256])
    b_gk = din("b_gk", [1, 256]); gla_norm = din("gla_norm", [128, 1])
    a_re = din("s5_a_re", [2048]); a_im = din("s5_a_im", [2048])
    b_re = din("s5_b_re", [2048, 16]); b_im = din("s5_b_im", [2048, 16])
    c_re = din("s5_c_re", [32, 16, 64]); c_im = din("s5_c_im", [32, 16, 64])
    s5_d = din("s5_d", [512]); log_dt = din("s5_log_dt", [1, 32])
    w_glu = din("w_s5_glu", [512, 512]); s5_norm = din("s5_norm", [512])
    w_o = din("w_o", [D, D]); g_post_mix = din("g_post_mix", [1, D]); g_pre_ffn = din("g_pre_ffn", [D])
    w_gate = din("w_gate", [D, DFF]); w_up = din("w_up", [D, DFF]); w_down = din("w_down", [DFF, D])
    g_post_ffn = din("g_post_ffn", [1, D])

    yp = dout("yp", [SEQ, D]); ys = dout("ys", [128, D]); glap = dout("glap", [4, 64, 128])
    s5rep = dout("s5rep", [16, 128]); s5imp = dout("s5imp", [16, 128])
    glas = dout("glas", [16, 4, 64, 128]); s5res = dout("s5res", [16, 2048]); s5ims = dout("s5ims", [16, 2048])
    x1d = nc.dram_tensor("x1_scratch", [SEQ + 128, D], F32, kind="Internal").ap()
    wg_b = nc.dram_tensor("wg_bf16", [D, DFF], BF16, kind="Internal").ap()
    wu_b = nc.dram_tensor("wu_bf16", [D, DFF], BF16, kind="Internal").ap()
    wd_b = nc.dram_tensor("wd_bf16", [DFF, D], BF16, kind="Internal").ap()

    S = Sched(nc)
    es = contextlib.ExitStack()
    NW = 53000
    arena_t = es.enter_context(nc.sbuf_tensor("arena", [128, NW], F32))
    AR = Arena(arena_t, NW)
    banks = [es.enter_context(nc.psum_tensor("bank%d" % i, [128, 512], F32)) for i in range(8)]
    bank_bufs = [Buf("bank%d" % i) for i in range(8)]
    bank_rr = [0]

    def pbank():
        i = bank_rr[0] % 7
        bank_rr[0] += 1
        return banks[i][:, :], bank_bufs[i]

    pe_last_bases = [frozenset([0])]

    def pe(mms, reads, writes, skip=False, selfwait=False):
        bases = frozenset(int(m[1].base_partition()) for m in mms)
        assert len(bases) == 1, "mixed PE row-tile positions in one group"
        if bases != pe_last_bases[0]:
            selfwait = True
        pe_last_bases[0] = bases
        def fn(e, mms=mms, skip=skip):
            ins = None
            for (o, l, r, st, sp_) in mms:
                if skip:
                    ins = e.matmul(o, lhsT=l, rhs=r, start=st, stop=sp_, skip_group_check=True)
                else:
                    ins = e.matmul(o, lhsT=l, rhs=r, start=st, stop=sp_)
            return ins
        return S.op("pe", fn, reads, writes, selfwait=selfwait)

    def act(out, in_, func, reads, writes, bias=None, scale=None, accum=None):
        kw = {}
        if bias is not None:
            kw["bias"] = bias
        if scale is not None:
            kw["scale"] = scale
        if accum is not None:
            kw["accum_out"] = accum
        return S.op("act", lambda e: e.activation(out, in_, func, **kw), reads, writes)

    def tt(eng, out, a, b, op, reads, writes):
        return S.op(eng, lambda e: e.tensor_tensor(out, a, b, op), reads, writes)

    def ts(eng, out, a, s1, s2, op0, op1, reads, writes):
        if s2 is None:
            return S.op(eng, lambda e: e.tensor_scalar(out, a, s1, None, op0), reads, writes)
        return S.op(eng, lambda e: e.tensor_scalar(out, a, s1, s2, op0, op1), reads, writes)

    def stt(eng, out, in0, sc, in1, op0, op1, reads, writes):
        return S.op(eng, lambda e: e.scalar_tensor_tensor(out, in0, sc, in1, op0, op1), reads, writes)

    def cp(eng, out, in_, reads, writes):
        if eng == "act":
            return S.op("act", lambda e: e.copy(out, in_), reads, writes)
        return S.op(eng, lambda e: e.tensor_copy(out, in_), reads, writes)

    def memset(eng, ap, val, writes):
        return S.op(eng, lambda e: e.memset(ap, val), (), writes)

    def dma(eng, out, in_, reads, writes, chan, slow=False):
        if slow:
            return S.dma(eng, lambda e: e.dma_start(out=out, in_=in_, allow_slow_non_contiguous=True),
                         reads, writes, chan)
        return S.dma(eng, lambda e: e.dma_start(out=out, in_=in_), reads, writes, chan)

    def asel(ap, pattern, base, cm, buf, op=ALU.is_ge, fill=0.0):
        return S.op("pool", lambda e: e.affine_select(out=ap, in_=ap, pattern=pattern, compare_op=op,
                                                     fill=fill, base=base, channel_multiplier=cm),
                    [buf], [buf])

    MV = [0, 1, 2, 3, 4, 8, 12, 16, 20, 24, 28, 32]
    NM = len(MV)
    identf = AR.alloc([128], F32); b_identf = Buf("identf")
    identb = AR.alloc([128], BF16); b_identb = Buf("identb")
    onesb = AR.alloc([128], BF16); b_onesb = Buf("onesb")
    masks = {}
    for ty in ("p", "s"):
        masks[ty] = dict(
            cs=AR.alloc([128], F32), rev=AR.alloc([128], F32), att=AR.alloc([128], F32),
            col=AR.alloc([16], F32), col1=AR.alloc([16], F32), buf=Buf("mask" + ty))
    gpm_row = AR.alloc([D], F32); b_gpm_row = Buf("gpm_row")
    gpm_col = AR.alloc([8], F32); gpf_col = AR.alloc([8], F32); b_gcol = Buf("gcol")
    glan = AR.alloc([1], F32); dsk = AR.alloc([4], F32); s5n = AR.alloc([4], F32); b_vec = Buf("vec")
    wgk = AR.alloc([256], F32, parts=32); b_wgk = Buf("wgk")
    APWrr = AR.alloc([8, 16, 2], F32); APWii = AR.alloc([8, 16, 2], F32); b_A8 = Buf("A8")
    Sg = AR.alloc([4, 128], F32, parts=64); b_Sg = Buf("Sg")
    Sgb = AR.alloc([4, 128], BF16, parts=64); b_Sgb = Buf("Sgb")
    Hcar = AR.alloc([16, 2], F32); b_Hcar = Buf("Hcar")
    eps_t = AR.alloc([1], F32); b_eps = Buf("eps")
    PERSIST_MARK = AR.mark()
    BPT = AR.alloc([4, 2, 2, T5, 128], BF16); b_BPT = Buf("BPT")
    CAT = AR.alloc([16, T5, 2, 64], BF16); b_CAT = Buf("CAT")
    KT = AR.alloc([4, T5, 128], BF16); b_KT = Buf("KT")
    win = AR.alloc([8, DIN], BF16); b_win = Buf("win")
    wglu = AR.alloc([4, 512], BF16); b_wglu = Buf("wglu")
    wo = AR.alloc([8, D], BF16); b_wo = Buf("wo")
    W_END = AR.mark()

    memset("pool", identf, 1.0, [b_identf])
    asel(identf, [[-1, 128]], 0, 1, b_identf, op=ALU.is_equal)
    cp("pool", identb, identf, [b_identf], [b_identb])
    memset("pool", onesb, 1.0, [b_onesb])
    memset("pool", Sg, 0.0, [b_Sg])
    memset("pool", Sgb, 0.0, [b_Sgb])
    memset("pool", Hcar, 0.0, [b_Hcar])
    memset("pool", eps_t, EPS, [b_eps])

    def build_masks(ty, cs_):
        m = masks[ty]
        mb = m["buf"]
        ncn = 128 // cs_
        for key, val in (("cs", -1.0 / 16), ("rev", -1.0 / 16), ("att", 1.0)):
            ap = m[key]
            memset("pool", ap, val, [mb])
            v3 = ap.rearrange("p (c i) -> p c i", c=ncn)
            asel(v3, [[-cs_, ncn], [0, cs_]], 0, 1, mb)
            asel(v3, [[cs_, ncn], [0, cs_]], cs_ - 1, -1, mb)
            if key == "rev":
                asel(ap, [[-1, 128]], -1, 1, mb)
            else:
                asel(ap, [[1, 128]], 0, -1, mb)
        nci = min(ncn, 16)
        for key, val in (("col", -1.0 / 16), ("col1", 1.0)):
            ap = m[key]
            memset("pool", ap, val, [mb])
            asel(ap[:, 0:nci], [[-cs_, nci]], 0, 1, mb)
            asel(ap[:, 0:nci], [[cs_, nci]], cs_ - 1, -1, mb)


    stg = AR.alloc([768], F32, parts=16); b_stg = Buf("stg")
    memset("pool", stg, 0.0, [b_stg])
    dma("sp", stg[0:8, 0:128], g_pre_mix.rearrange("(k p) -> k p", p=128), [], [b_stg], b_stg)
    dma("sp", stg[0:8, 128:256], g_pre_ffn.rearrange("(k p) -> k p", p=128), [], [b_stg], b_stg)
    dma("sp", stg[0:4, 256:384], s5_d.rearrange("(k p) -> k p", p=128), [], [b_stg], b_stg)
    dma("sp", stg[0:4, 384:512], s5_norm.rearrange("(k p) -> k p", p=128), [], [b_stg], b_stg)
    dma("sp", stg[0:16, 512:640], a_re.rearrange("(k p) -> k p", p=128), [], [b_stg], b_stg)
    dma("sp", stg[0:16, 640:768], a_im.rearrange("(k p) -> k p", p=128), [], [b_stg], b_stg)
    dma("sp", glan, gla_norm, [], [b_vec], b_vec)
    bk0, bb0 = pbank()
    pe([(bk0[:, 16 * j:16 * j + 16], stg[0:16, 128 * j:128 * j + 128], identf[0:16, 0:16], True, True)
        for j in range(6)], [b_stg, b_identf], [bb0])
    cp("dve", gpm_col, bk0[:, 0:8], [bb0], [b_gcol])
    cp("dve", gpf_col, bk0[:, 16:24], [bb0], [b_gcol])
    cp("dve", dsk, bk0[:, 32:36], [bb0], [b_vec])
    cp("dve", s5n, bk0[:, 48:52], [bb0], [b_vec])
    dma("sp", wgk[0:16, :], w_gk2, [], [b_wgk], b_wgk)
    dma("sp", wgk[16:17, :], b_gk, [], [b_wgk], b_wgk)
    dma("sp", gpm_row, g_post_mix.partition_broadcast(128), [], [b_gpm_row], b_gpm_row)

    wst = [AR.alloc([DIN], F32) for _ in range(3)]; b_wst = [Buf("wst%d" % i) for i in range(3)]
    wst_i = [0]
    cast_engs = ["act", "dve", "act"]

    def load_cast(dst, b_dst, src_rows, ncols, scale_col=None):
        i = wst_i[0] % 3
        wst_i[0] += 1
        st_, bst = wst[i], b_wst[i]
        dma("sp", st_[:, 0:ncols], src_rows, [], [bst], bst)
        eng = cast_engs[i]
        if scale_col is None:
            cp(eng, dst, st_[:, 0:ncols], [bst], [b_dst])
        elif eng == "act":
            S.op("act", lambda e: e.mul(dst, st_[:, 0:ncols], scale_col), [bst, b_gcol], [b_dst])
        else:
            ts(eng, dst, st_[:, 0:ncols], scale_col, None, ALU.mult, None, [bst, b_gcol], [b_dst])

    b_tb = Buf("tb")
    are = AR.alloc([16], F32); aim = AR.alloc([16], F32); ldt_all = AR.alloc([32], F32)
    dtt = AR.alloc([16], F32); lr = AR.alloc([16], F32); lrdt = AR.alloc([16], F32); th = AR.alloc([16], F32)
    mvals = AR.alloc([NM, 16], F32); MAG = AR.alloc([NM, 16], F32); ANG = AR.alloc([NM, 16], F32)
    RS = AR.alloc([NM, 16], F32); RC = AR.alloc([NM, 16], F32); KF = AR.alloc([NM, 16], F32)
    KI = AR.alloc([NM, 16], I32); PWre = AR.alloc([NM, 16], F32); PWim = AR.alloc([NM, 16], F32)
    t16a = AR.alloc([16], F32); t16b = AR.alloc([16], F32); t16c = AR.alloc([16], F32)
    fre = AR.alloc([16], F32); fim = AR.alloc([16], F32)
    Bre = AR.alloc([16, 16], F32); Bim = AR.alloc([16, 16], F32)
    bbre = AR.alloc([16, 16], F32); bbim = AR.alloc([16, 16], F32); t256 = AR.alloc([16, 16], F32)
    Cin_re = AR.alloc([2, 128], F32); Cin_im = AR.alloc([2, 128], F32)
    Cre = AR.alloc([16, 16], F32); Cim = AR.alloc([16, 16], F32)
    ABre = AR.alloc([T5, 16, 16], F32); ABim = AR.alloc([T5, 16, 16], F32)
    CAre = AR.alloc([T5 + 1, 16, 16], F32); CAimN = AR.alloc([T5 + 1, 16, 16], F32); tCA = AR.alloc([T5 + 1, 16, 16], F32)
    tAB = tCA[:, 0:T5, :, :]
    mask2 = AR.alloc([2], F32); parm = AR.alloc([2], F32); rowm = AR.alloc([8], F32); tmask = AR.alloc([2], F32)
    Xb = [AR.alloc([16, 2, 16], F32) for _ in range(2)]; b_Xb = [Buf("Xb0"), Buf("Xb1")]
    bbpad_re = AR.alloc([16, 128], F32); bbpad_im = AR.alloc([16, 128], F32); b_bbpad = Buf("bbpad")

    cp("dve", are, bk0[:, 64:80], [bb0], [b_tb])
    cp("dve", aim, bk0[:, 80:96], [bb0], [b_tb])
    dma("sp", ldt_all, log_dt.partition_broadcast(128), [], [b_tb], b_tb)
    dma("sp", Bre, b_re.rearrange("(r p) j -> p r j", p=128), [], [b_tb], b_tb)
    dma("act", Bim, b_im.rearrange("(r p) j -> p r j", p=128), [], [b_tb], b_tb)
    for (cin, csrc) in ((Cin_re, c_re), (Cin_im, c_im)):
        cv = csrc.rearrange("(blk prl g2) j n -> blk g2 prl j n", blk=2, prl=8, g2=2)
        for blk in range(2):
            for g2_ in range(2):
                dma("sp", cin[:, blk, g2_ * 64:(g2_ + 1) * 64], cv[blk, g2_], [], [b_tb], b_tb)

    for kt in range(8):
        load_cast(win[:, kt, :], b_win, w_in[kt * 128:(kt + 1) * 128, :], DIN, gpm_col[:, kt:kt + 1])
    for kt in range(4):
        load_cast(wglu[:, kt, :], b_wglu, w_glu[kt * 128:(kt + 1) * 128, :], 512)
    for kt in range(8):
        load_cast(wo[:, kt, :], b_wo, w_o[kt * 128:(kt + 1) * 128, :], D)

    def range_col(ap_col, lo, hi):
        memset("pool", ap_col, 1.0, [b_tb])
        asel(ap_col, [[0, 1]], -lo, 1, b_tb)
        asel(ap_col, [[0, 1]], hi - 1, -1, b_tb)
    range_col(mask2[:, 0:1], 0, 64)
    range_col(mask2[:, 1:2], 64, 128)
    range_col(parm[:, 0:1], 0, 32)
    range_col(tmask[:, 0:1], 64, 96)
    tt("pool", parm[:, 0:1], parm[:, 0:1], tmask[:, 0:1], ALU.add, [b_tb], [b_tb])
    range_col(parm[:, 1:2], 32, 64)
    range_col(tmask[:, 1:2], 96, 128)
    tt("pool", parm[:, 1:2], parm[:, 1:2], tmask[:, 1:2], ALU.add, [b_tb], [b_tb])
    for m_ in range(8):
        range_col(rowm[:, m_:m_ + 1], 16 * m_, 16 * m_ + 16)
    for mi, m_ in enumerate(MV):
        memset("pool", mvals[:, mi, :], float(m_), [b_tb])
    build_masks("p", 64)
    build_masks("s", 8)

    R, W = [b_tb], [b_tb]
    ldv = ldt_all.rearrange("p (r g) -> p r g", g=2)
    cp("dve", dtt[0:64, :], ldv[0:64, :, 0], R, W)
    cp("dve", dtt[64:128, :], ldv[64:128, :, 1], R, W)
    act(dtt, dtt, AF.Exp, R, W)
    ts("dve", lr, are, -1e-4, None, ALU.min, None, R, W)
    tt("dve", lrdt, lr, dtt, ALU.mult, R, W)
    tt("dve", th, aim, dtt, ALU.mult, R, W)
    b9 = lambda a: a.unsqueeze(1).to_broadcast([128, NM, 16])
    tt("dve", MAG, mvals, b9(lrdt), ALU.mult, R, W)
    act(MAG, MAG, AF.Exp, R, W)
    tt("dve", ANG, mvals, b9(th), ALU.mult, R, W)

    def range_reduce(dst, shift):
        ts("dve", dst, ANG, float(shift), None, ALU.add, None, R, W)
        ts("dve", KF, dst, float(1.0 / TWO_PI), None, ALU.mult, None, R, W)
        cp("dve", KI, KF, R, W)
        cp("dve", KF, KI, R, W)
        stt("dve", dst, KF, float(-TWO_PI_HI), dst, ALU.mult, ALU.add, R, W)
        stt("dve", dst, KF, float(-TWO_PI_LO), dst, ALU.mult, ALU.add, R, W)
        ts("dve", KF, dst, float(math.pi), float(-TWO_PI), ALU.is_gt, ALU.mult, R, W)
        tt("dve", dst, dst, KF, ALU.add, R, W)
        ts("dve", dst, dst, float(math.pi), float(-math.pi), ALU.min, ALU.max, R, W)
    range_reduce(RS, TWO_PI)
    range_reduce(RC, TWO_PI + math.pi / 2)
    act(RS, RS, AF.Sin, R, W)
    act(RC, RC, AF.Sin, R, W)
    tt("dve", PWre, MAG, RC, ALU.mult, R, W)
    tt("dve", PWim, MAG, RS, ALU.mult, R, W)
    ts("dve", t16a, PWre[:, 1, :], -1.0, None, ALU.add, None, R, W)
    tt("dve", t16b, lr, lr, ALU.mult, R, W)
    tt("dve", t16c, aim, aim, ALU.mult, R, W)
    tt("dve", t16b, t16b, t16c, ALU.add, R, W)
    S.op("dve", lambda e: e.reciprocal(t16b, t16b), R, W)
    tt("dve", fre, t16a, lr, ALU.mult, R, W)
    tt("dve", t16c, PWim[:, 1, :], aim, ALU.mult, R, W)
    tt("dve", fre, fre, t16c, ALU.add, R, W)
    tt("dve", fre, fre, t16b, ALU.mult, R, W)
    tt("dve", fim, PWim[:, 1, :], lr, ALU.mult, R, W)
    tt("dve", t16c, t16a, aim, ALU.mult, R, W)
    tt("dve", fim, fim, t16c, ALU.subtract, R, W)
    tt("dve", fim, fim, t16b, ALU.mult, R, W)
    bj = lambda a: a.unsqueeze(2).to_broadcast([128, 16, 16])
    tt("dve", bbre, Bre, bj(fre), ALU.mult, R, W)
    tt("dve", t256, Bim, bj(fim), ALU.mult, R, W)
    tt("dve", bbre, bbre, t256, ALU.subtract, R, W)
    tt("dve", bbim, Bim, bj(fre), ALU.mult, R, W)
    tt("dve", t256, Bre, bj(fim), ALU.mult, R, W)
    tt("dve", bbim, bbim, t256, ALU.add, R, W)
    cp("dve", APWrr[:, :, :, 0], PWre[:, 4:12, :], R, [b_A8])
    cp("dve", APWrr[:, :, :, 1], PWre[:, 4:12, :], R, [b_A8])
    ts("dve", APWii[:, :, :, 0], PWim[:, 4:12, :], -1.0, None, ALU.mult, None, R, [b_A8])
    cp("dve", APWii[:, :, :, 1], PWim[:, 4:12, :], R, [b_A8])
    for (cin, cdst) in ((Cin_re, Cre), (Cin_im, Cim)):
        bk, bb_ = pbank()
        pe([(bk[:, blk * 128:(blk + 1) * 128], cin[:, blk, :], identf, True, True) for blk in range(2)],
           [b_tb, b_identf], [bb_])
        cp("dve", cdst.rearrange("p (b r) j -> p b (r j)", b=2), bk[:, 0:256].rearrange("p (b x) -> p b x", b=2),
           [bb_], W)

    def pwb(pw, n):
        return pw[:, 0:n, :].unsqueeze(3).to_broadcast([128, n, 16, 16])

    def xb(a, n):
        return a.unsqueeze(1).to_broadcast([128, n, 16, 16])
    tt("dve", ABre, xb(bbre, T5), pwb(PWre, T5), ALU.mult, R, W)
    tt("dve", tAB, xb(bbim, T5), pwb(PWim, T5), ALU.mult, R, W)
    tt("dve", ABre, ABre, tAB, ALU.subtract, R, W)
    tt("dve", ABim, xb(bbim, T5), pwb(PWre, T5), ALU.mult, R, W)
    tt("dve", tAB, xb(bbre, T5), pwb(PWim, T5), ALU.mult, R, W)
    tt("dve", ABim, ABim, tAB, ALU.add, R, W)
    tt("dve", CAre, xb(Cre, T5 + 1), pwb(PWre, T5 + 1), ALU.mult, R, W)
    tt("dve", tCA, xb(Cim, T5 + 1), pwb(PWim, T5 + 1), ALU.mult, R, W)
    tt("dve", CAre, CAre, tCA, ALU.subtract, R, W)
    tt("dve", CAimN, xb(Cre, T5 + 1), pwb(PWim, T5 + 1), ALU.mult, R, W)
    tt("dve", tCA, xb(Cim, T5 + 1), pwb(PWre, T5 + 1), ALU.mult, R, W)
    tt("dve", CAimN, CAimN, tCA, ALU.add, R, W)
    ts("dve", CAimN, CAimN, -1.0, None, ALU.mult, None, R, W)

    xi = 0
    for reim, AB in ((0, ABre), (1, ABim)):
        for tau in range(T5):
            X = Xb[xi % 2]; bX = b_Xb[xi % 2]; xi += 1
            for g2 in range(2):
                ts("dve", X[:, :, g2, :], AB[:, tau, :, :], mask2[:, g2:g2 + 1], None, ALU.mult, None,
                   [b_tb], [bX])
            bk, bb_ = pbank()
            pe([(bk[:, ft * 128:(ft + 1) * 128],
                 X[:, 4 * ft:4 * ft + 4, :, :].rearrange("p a b c -> p (a b c)"), identf, True, True)
                for ft in range(4)], [bX, b_identf], [bb_])
            for pp in range(2):
                if pp == 0:
                    S.op("act", lambda e, o=BPT[:, :, pp, reim, tau, :], i=bk.rearrange("p (f x) -> p f x", f=4),
                         m=parm[:, pp:pp + 1]: e.mul(o, i, m), [bb_, b_tb], [b_BPT])
                else:
                    ts("dve", BPT[:, :, pp, reim, tau, :], bk.rearrange("p (f x) -> p f x", f=4),
                       parm[:, pp:pp + 1], None, ALU.mult, None, [bb_, b_tb], [b_BPT])

    memset("pool", CAT.rearrange("p a b c d -> p (a b c d)"), 0.0, [b_CAT])
    for q in range(2):
        for g2 in range(2):
            for reim, CA in ((0, CAre), (1, CAimN)):
                o = CAT[:, q::2, :, reim, q * 32 + g2 * 16:q * 32 + g2 * 16 + 16]
                i = CA[:, 1:T5 + 1, q::2, :].rearrange("p s r j -> p r s j")
                ts("dve", o, i, mask2[:, g2:g2 + 1], None, ALU.mult, None, [b_tb], [b_CAT])

    memset("pool", bbpad_re.rearrange("p a b -> p (a b)"), 0.0, [b_bbpad])
    memset("pool", bbpad_im.rearrange("p a b -> p (a b)"), 0.0, [b_bbpad])
    for (bp, bsrc) in ((bbpad_re, bbre), (bbpad_im, bbim)):
        for prl in range(4):
            for g2 in range(2):
                c0 = prl * 32 + g2 * 16
                cp("dve", bp[64 * g2:64 * g2 + 64, prl::4, c0:c0 + 16], bsrc[64 * g2:64 * g2 + 64, prl::4, :],
                   [b_tb], [b_bbpad])
    for ft in range(4):
        bk, bb_ = pbank()
        mms = []
        kv = bk[:, 0:16 * T5].rearrange("p (t j) -> p t j", t=T5)
        for prl in range(4):
            pr = 4 * ft + prl
            for k_, (bp, CA) in enumerate(((bbpad_re, CAre), (bbpad_im, CAimN))):
                mms.append((kv, bp[:, pr, :], CA[:, 0:T5, pr, :], (prl == 0 and k_ == 0), (prl == 3 and k_ == 1)))
        pe(mms, [b_bbpad, b_tb], [bb_])
        for m_ in range(8):
            ts("dve", KT[:, ft, :, m_ * 16:(m_ + 1) * 16], kv, rowm[:, m_:m_ + 1], None, ALU.mult, None,
               [bb_, b_tb], [b_KT])

    S.barrier()
    AR.reset(W_END)
    b_cvg = Buf("cvg"); b_cvu = Buf("cvu"); b_cvd = Buf("cvd")
    for kt in range(8):
        rs_ = slice(kt * 128, (kt + 1) * 128)
        dma("pool", wg_b[rs_, :], w_gate[rs_, :], [], [b_cvg], b_cvg)
        dma("pool", wu_b[rs_, :], w_up[rs_, :], [], [b_cvu], b_cvu)
    for kt in range(NFF):
        rs_ = slice(kt * 128, (kt + 1) * 128)
        dma("pool", wd_b[rs_, :], w_down[rs_, :], [], [b_cvd], b_cvd)

    BT = 256
    NCH = BT // T5
    hTs = [AR.alloc([8, BT], BF16) for _ in range(2)]; b_hTs = [Buf("hT0"), Buf("hT1")]
    uTs = [AR.alloc([4, BT], BF16) for _ in range(2)]; b_uTs = [Buf("uT0"), Buf("uT1")]
    mixT = AR.alloc([8, BT], BF16); b_mixT = Buf("mixT")
    HL = AR.alloc([72 * 32], F32); b_HL = Buf("HL")
    Hpb = AR.alloc([NCH, 16, 2], BF16); b_Hpb = Buf("Hpb")
    xt = [AR.alloc([D], F32) for _ in range(2)]; b_xt = [Buf("xt0"), Buf("xt1")]
    hn0 = AR.alloc([D], BF16); hn = [hn0, hn0]; b_hn0 = Buf("hn0"); b_hn = [b_hn0, b_hn0]
    st = [AR.alloc([8], F32) for _ in range(2)]; b_st = [Buf("st0"), Buf("st1")]
    st2 = [AR.alloc([8], F32) for _ in range(2)]; b_st2 = [Buf("st20"), Buf("st21")]
    qTs = [AR.alloc([4, BT], BF16, parts=64) for _ in range(2)]; b_qTs = [Buf("qT0"), Buf("qT1")]
    kTs = [AR.alloc([4, BT], BF16, parts=64) for _ in range(2)]; b_kTs = [Buf("kT0"), Buf("kT1")]
    sgs = [AR.alloc([4, BT], BF16) for _ in range(2)]; b_sgs = [Buf("sg0"), Buf("sg1")]
    gkTs = [AR.alloc([BT], F32, parts=32) for _ in range(2)]; b_gkTs = [Buf("gkT0"), Buf("gkT1")]
    for g_, bg_ in zip(gkTs, b_gkTs):
        memset("dve", g_, 1.0, [bg_])
    vb = AR.alloc([512], BF16); b_vb = Buf("vb")
    spt = AR.alloc([256], F32); b_spt = Buf("spt")
    ebt = AR.alloc([4, 128], F32, parts=64); b_ebt = Buf("ebt")
    enbt = AR.alloc([4, 128], F32, parts=64); b_enbt = Buf("enbt")
    qd = AR.alloc([4, 128], BF16, parts=64); b_qd = Buf("qd")
    ki = AR.alloc([4, 128], BF16, parts=64); b_ki = Buf("ki")
    eet = AR.alloc([256], F32); b_eet = Buf("eet")
    kend = AR.alloc([256], BF16); b_kend = Buf("kend")
    kendm = [AR.alloc([256], BF16) for _ in range(2)]; b_kendm = [Buf("kendm0"), Buf("kendm1")]
    ebl = AR.alloc([4, 16], F32, parts=64); b_ebl = Buf("ebl")
    attT = AR.alloc([4, 128], BF16); b_attT = Buf("attT")
    osq = AR.alloc([4, 128], BF16); b_osq = Buf("osq")
    rst = AR.alloc([4, 128], F32); b_rst = Buf("rst")
    S0 = [AR.alloc([4, 128], F32, parts=64) for _ in range(2)]; b_S0 = [Buf("S00"), Buf("S01")]
    S0b = [AR.alloc([4, 128], BF16, parts=64) for _ in range(2)]; b_S0b = [Buf("S0b0"), Buf("S0b1")]
    S1 = [AR.alloc([4, 128], F32, parts=64) for _ in range(2)]; b_S1 = [Buf("S10"), Buf("S11")]
    yy = AR.alloc([2, 4, BT], F32)
    y5 = yy[:, 0, :, :]; b_y5 = Buf("y5")
    y5g = yy[:, 1, :, :]; b_y5g = Buf("y5g")
    y5gb = AR.alloc([4, BT], BF16); b_y5gb = Buf("y5gb")
    tA = AR.alloc([4, BT], F32); b_tA = Buf("tA")
    tB = AR.alloc([BT], F32); b_tB = Buf("tB")
    sqb = y5gb; b_sqb = b_y5gb
    scP = AR.alloc([16, 16, 2], F32); scQ = AR.alloc([16, 16, 2], F32); b_sc = Buf("sc")
    sgtmp = AR.alloc([BT], F32); b_sgtmp = Buf("sgtmp")
    h0in = yy.rearrange("p a b c -> p (a b c)")[0:16, :]; b_h0in = Buf("h0in")
    s5o = h0in; b_s5o = b_h0in
    Hout = AR.alloc([2, 16], F32); b_Hout = Buf("Hout")

    xt_i = [0]

    def load_norm_transpose(src_ap, ntok, dstT, b_dstT, col0, hnslot):
        i = xt_i[0] % 2
        xt_i[0] += 1
        X, bX = xt[i], b_xt[i]
        dma("sp", X[0:ntok, :], src_ap, [], [bX], bX)
        H_, bH = hn[hnslot], b_hn[hnslot]
        s_, bs = st[hnslot], b_st[hnslot]
        act(H_[0:ntok, :], X[0:ntok, :], AF.Square, [bX], [bH, bs], accum=s_[0:ntok, 0:1])
        act(s_[0:ntok, 1:2], s_[0:ntok, 0:1], AF.Ln, [bs, b_eps], [bs], bias=eps_t[0:ntok, :], scale=1.0 / D)
        act(s_[0:ntok, 2:3], s_[0:ntok, 1:2], AF.Exp, [bs], [bs], scale=-0.5)
        S.op("act", lambda e: e.mul(H_[0:ntok, :], X[0:ntok, :], s_[0:ntok, 2:3]), [bX, bs], [bH])
        for half in range(2):
            bk, bb_ = pbank()
            pe([(bk[:, j * 128:j * 128 + ntok], H_[0:ntok, (half * 4 + j) * 128:(half * 4 + j + 1) * 128],
                 identb[0:ntok, 0:ntok], True, True) for j in range(4)], [bH, b_identb], [bb_])
            o = dstT[:, half * 4:half * 4 + 4, col0:col0 + ntok]
            iv = bk.rearrange("p (j x) -> p j x", j=4)[:, :, 0:ntok]
            cp("act", o, iv, [bb_], [b_dstT])

    class Blk:
        pass

    def mk_block(idx, kind, tiles, x1row0):
        b = Blk()
        b.idx, b.kind, b.tiles, b.x1row0 = idx, kind, tiles, x1row0
        b.ntb = sum(t[1] for t in tiles)
        b.nch = b.ntb // T5
        b.need_out = kind != "meta"
        b.mty = masks["s" if kind == "sample" else "p"]
        b.uT = uTs[idx % 2]; b.b_uT = b_uTs[idx % 2]
        b.hT = hTs[idx % 2]; b.b_hT = b_hTs[idx % 2]
        b.qT = qTs[idx % 2]; b.b_qT = b_qTs[idx % 2]
        b.kT = kTs[idx % 2]; b.b_kT = b_kTs[idx % 2]
        b.sg = sgs[idx % 2]; b.b_sg = b_sgs[idx % 2]
        b.gkT = gkTs[idx % 2]; b.b_gkT = b_gkTs[idx % 2]
        if kind == "sample":
            b.G, b.L = 16, 2
        elif kind == "meta":
            b.G, b.L = 1, b.nch
        else:
            b.G, b.L = b.nch // 8, 8
        b.HLv = HL[:, 0:b.G * (b.L + 1) * 32].rearrange("p (g l r i) -> p g l r i", g=b.G, l=b.L + 1, r=16)
        return b

    def stageA(b):
        ntb = b.ntb
        hT, b_hT, qT, b_qT, kT, b_kT = b.hT, b.b_hT, b.qT, b.b_qT, b.kT, b.b_kT
        sg, b_sg, gkT, b_gkT = b.sg, b.b_sg, b.gkT, b.b_gkT
        col = 0
        for ti, (src, ntok) in enumerate(b.tiles):
            load_norm_transpose(src, ntok, hT, b_hT, col, 0)
            col += ntok
            yield

        def fm(c0, M):
            bk, bb_ = pbank()
            pe([(bk[0:M, 0:ntb], win[:, kt, c0:c0 + M], hT[:, kt, 0:ntb], kt == 0, kt == 7) for kt in range(8)],
               [b_win, b_hT], [bb_])
            return bk, bb_
        bk, bb_ = fm(1536, 16)
        cp("act", gkT[0:16, 0:ntb], bk[0:16, 0:ntb], [bb_], [b_gkT])
        yield
        for ft in range(4):
            bk, bb_ = fm(1552 + 128 * ft, 128)
            cp("act", b.uT[:, ft, 0:ntb], bk[:, 0:ntb], [bb_], [b.b_uT])
            yield
        if b.need_out:
            for h in range(4):
                bk, bb_ = fm(64 * h, 64)
                cp("act", qT[:, h, 0:ntb], bk[0:64, 0:ntb], [bb_], [b_qT])
                bk, bb_ = fm(256 + 64 * h, 64)
                cp("act", kT[:, h, 0:ntb], bk[0:64, 0:ntb], [bb_], [b_kT])
                yield
            for h in range(4):
                bk, bb_ = fm(1024 + 128 * h, 128)
                act(sgtmp[:, 0:ntb], bk[:, 0:ntb], AF.Exp, [bb_], [b_sgtmp], scale=-1.0)
                act(sgtmp[:, 0:ntb], sgtmp[:, 0:ntb], AF.Ln, [b_sgtmp], [b_sgtmp], bias=1.0)
                act(sgtmp[:, 0:ntb], sgtmp[:, 0:ntb], AF.Exp, [b_sgtmp], [b_sgtmp], scale=-1.0)
                tt("dve", sg[:, h, 0:ntb], sgtmp[:, 0:ntb], bk[:, 0:ntb], ALU.mult, [b_sgtmp, bb_], [b_sg])
                yield

    def stageB(b):
        kind, need_out, mty = b.kind, b.need_out, b.mty
        hT, b_hT, qT, b_qT, kT, b_kT = b.hT, b.b_hT, b.qT, b.b_qT, b.kT, b.b_kT
        sg, b_sg, gkT, b_gkT = b.sg, b.b_sg, b.gkT, b.b_gkT
        mb = mty["buf"]
        col = 0
        for ti, (src, ntok) in enumerate(b.tiles):
            c0 = col
            col += ntok
            if kind == "sample":
                chunks = [(8 * i, 8 * i + 8) for i in range(16)]
            elif kind == "meta":
                chunks = [(0, 16)]
            else:
                chunks = [(0, 64), (64, 128)]
            nci = len(chunks)
            bkk, bbk = pbank()
            pe([(bkk[0:ntok, 0:256], hT[:, kt, c0:c0 + ntok], win[:, kt, 256:512], kt == 0, kt == 7) for kt in range(8)],
               [b_win, b_hT], [bbk])
            bkv, bbv = pbank()
            pe([(bkv[0:ntok, :], hT[:, kt, c0:c0 + ntok], win[:, kt, 512:1024], kt == 0, kt == 7) for kt in range(8)],
               [b_win, b_hT], [bbv])
            cp("act", vb[0:ntok, :], bkv[0:ntok, :], [bbv], [b_vb])
            bkl, bbl = pbank()
            pe([(bkl[0:ntok, 0:256], gkT[0:17, c0:c0 + ntok], wgk[0:17, :], True, True)], [b_gkT, b_wgk], [bbl])
            act(spt[0:ntok, :], bkl[0:ntok, 0:256], AF.Exp, [bbl], [b_spt], scale=-1.0)
            act(spt[0:ntok, :], spt[0:ntok, :], AF.Ln, [b_spt], [b_spt], bias=1.0)
            bke, bbe = pbank()
            pe([(bke[0:ntok, 0:256], mty["rev"][0:ntok, 0:ntok], spt[0:ntok, :], True, True)], [mb, b_spt], [bbe])
            act(eet[0:ntok, :], bke[0:ntok, 0:256], AF.Exp, [bbe], [b_eet])
            tt("dve", kend[0:ntok, :], bkk[0:ntok, 0:256], eet[0:ntok, :], ALU.mult, [bbk, b_eet], [b_kend])
            yield
            bkb, bbb = pbank()
            pe([(bkb[0:64, h * 16:h * 16 + nci], spt[0:ntok, 64 * h:64 * h + 64], mty["col"][0:ntok, 0:nci], True, True)
                for h in range(4)], [mb, b_spt], [bbb])
            act(ebl[:, :, 0:nci], bkb[0:64, 0:64].rearrange("p (h c) -> p h c", h=4)[:, :, 0:nci], AF.Exp,
                [bbb], [b_ebl])
            if need_out:
                bkt, bbt = pbank()
                pe([(bkt[0:64, h * 128:h * 128 + ntok], spt[0:ntok, 64 * h:64 * h + 64], mty["cs"][0:ntok, 0:ntok],
                     True, True) for h in range(4)], [mb, b_spt], [bbt])
                bv = bkt[0:64, :].rearrange("p (h t) -> p h t", h=4)[:, :, 0:ntok]
                act(ebt[:, :, 0:ntok], bv, AF.Exp, [bbt], [b_ebt])
                act(enbt[:, :, 0:ntok], bv, AF.Exp, [bbt], [b_enbt], scale=-1.0)
                stt("dve", qd[:, :, 0:ntok], qT[:, :, c0:c0 + ntok], 0.125, ebt[:, :, 0:ntok], ALU.mult, ALU.mult,
                    [b_qT, b_ebt], [b_qd])
                yield
                tt("dve", ki[:, :, 0:ntok], kT[:, :, c0:c0 + ntok], enbt[:, :, 0:ntok], ALU.mult, [b_kT, b_enbt], [b_ki])
                yield
                bka, bba = pbank()
                pe([(bka[0:ntok, h * 128:h * 128 + ntok], ki[:, h, 0:ntok], qd[:, h, 0:ntok], True, True)
                    for h in range(4)], [b_ki, b_qd], [bba])
                tt("dve", attT[0:ntok, :, 0:ntok], bka[0:ntok, :].rearrange("p (h t) -> p h t", h=4)[:, :, 0:ntok],
                   mty["att"][0:ntok, 0:ntok].unsqueeze(1).to_broadcast([ntok, 4, ntok]), ALU.mult,
                   [bba, mb], [b_attT])
                yield
                bko, bbo = banks[7][:, :], bank_bufs[7]
                pe([(bko[:, h * 128:h * 128 + ntok], vb[0:ntok, 128 * h:128 * h + 128], attT[0:ntok, h, 0:ntok],
                     h == 0, False) for h in range(4)], [b_vb, b_attT], [bbo], skip=True)

            if kind != "sample":
                for ci, (a, b_) in enumerate(chunks):
                    if need_out:
                        pe([(bko[:, h * 128 + a:h * 128 + b_], Sgb[:, h, :], qd[:, h, a:b_], False, ci == nci - 1)
                            for h in range(4)], [b_Sgb, b_qd], [bbo], skip=True)
                    km = kendm[ci % 2]; bkm = b_kendm[ci % 2]
                    S.op("act", lambda e, o=km[0:ntok, :], i_=kend[0:ntok, :], m_=mty["col1"][0:ntok, ci:ci + 1]: e.mul(o, i_, m_),
                         [b_kend, mb], [bkm])
                    bks, bbs = pbank()
                    pe([(bks[0:64, h * 128:(h + 1) * 128], km[0:ntok, 64 * h:64 * h + 64], vb[0:ntok, 128 * h:128 * h + 128],
                         True, True) for h in range(4)], [bkm, b_vb], [bbs])
                    for h in range(4):
                        stt("dve", Sg[:, h, :], Sg[:, h, :], ebl[:, h, ci:ci + 1], bks[0:64, h * 128:(h + 1) * 128],
                            ALU.mult, ALU.add, [b_Sg, b_ebl, bbs], [b_Sg])
                        yield
                    cp("act", Sgb.rearrange("p a b -> p (a b)"), Sg.rearrange("p a b -> p (a b)"), [b_Sg], [b_Sgb])
            else:
                for i in range(16):
                    a, b_ = chunks[i]
                    s0, bs0 = S0[i % 2], b_S0[i % 2]
                    s0b, bs0b = S0b[i % 2], b_S0b[i % 2]
                    s1, bs1 = S1[i % 2], b_S1[i % 2]
                    dma("sp", s0, sgla[i].rearrange("h d v -> d h v"), [], [bs0], bs0)
                    cp("act", s0b.rearrange("p a b -> p (a b)"), s0.rearrange("p a b -> p (a b)"), [bs0], [bs0b])
                    pe([(bko[:, h * 128 + a:h * 128 + b_], s0b[:, h, :], qd[:, h, a:b_], False, i == 15)
                        for h in range(4)], [bs0b, b_qd], [bbo], skip=True)
                    km = kendm[i % 2]; bkm = b_kendm[i % 2]
                    S.op("act", lambda e, o=km[0:ntok, :], i_=kend[0:ntok, :], m_=mty["col1"][0:ntok, i:i + 1]: e.mul(o, i_, m_),
                         [b_kend, mb], [bkm])
                    bks, bbs = pbank()
                    pe([(bks[0:64, h * 128:(h + 1) * 128], km[0:ntok, 64 * h:64 * h + 64], vb[0:ntok, 128 * h:128 * h + 128],
                         True, True) for h in range(4)], [bkm, b_vb], [bbs])
                    for h in range(4):
                        stt("dve", s1[:, h, :], s0[:, h, :], ebl[:, h, i:i + 1], bks[0:64, h * 128:(h + 1) * 128],
                            ALU.mult, ALU.add, [bs0, b_ebl, bbs], [bs1])
                        yield
                    dma("sp", glas[i].rearrange("h d v -> d h v"), s1, [bs1], [], bs1)

            if need_out:
                ov = bko.rearrange("p (h t) -> p h t", h=4)[:, :, 0:ntok]
                act(osq[:, :, 0:ntok], ov, AF.Square, [bbo], [b_osq])
                bkq, bbq = pbank()
                pe([(bkq[:, h * 128:h * 128 + ntok], onesb, osq[:, h, 0:ntok], True, True) for h in range(4)],
                   [b_onesb, b_osq], [bbq])
                qv = bkq.rearrange("p (h t) -> p h t", h=4)[:, :, 0:ntok]
                act(rst[:, :, 0:ntok], qv, AF.Ln, [bbq, b_eps], [b_rst], bias=eps_t, scale=1.0 / 128)
                act(rst[:, :, 0:ntok], rst[:, :, 0:ntok], AF.Exp, [b_rst], [b_rst], scale=-0.5)
                yield
                tt("dve", rst[:, :, 0:ntok], ov, rst[:, :, 0:ntok], ALU.mult, [bbo, b_rst], [b_rst])
                yield
                stt("dve", mixT[:, 0:4, c0:c0 + ntok], rst[:, :, 0:ntok], glan[:, 0:1], sg[:, :, c0:c0 + ntok],
                    ALU.mult, ALU.mult, [b_rst, b_vec, b_sg], [b_mixT])
                yield

    def stageC(b):
        ntb, nch, G, L = b.ntb, b.nch, b.G, b.L
        for bq in range(4):
            hf, fth = bq // 2, bq % 2
            bk, bb_ = pbank()
            mms = []
            for slot in range(8):
                ft = 2 * fth + slot // 4
                q = (slot // 2) % 2
                reim = slot % 2
                for tau in range(T5):
                    s_ = T5 - 1 - tau
                    mms.append((bk[:, slot * 64:slot * 64 + nch],
                                BPT[64 * hf:64 * hf + 64, ft, q, reim, tau, :],
                                b.uT[64 * hf:64 * hf + 64, ft, s_:ntb:T5], tau == 0, tau == T5 - 1))
            pe(mms, [b_BPT, b.b_uT], [bb_])
            for bsel in range(2):
                src = bk[:, bsel * 256:(bsel + 1) * 256].rearrange("p (x c) -> p x c", x=4)[:, :, 0:nch]
                src = src.rearrange("p x (g l) -> p x g l", g=G)
                pr0 = 4 * (2 * fth + bsel) + 2 * hf
                o = b.HLv[:, :, 1:L + 1, pr0:pr0 + 2, :].rearrange("p g l r i -> p (r i) g l")
                cp("act" if bsel else "dve", o, src, [bb_], [b_HL])

    def cmul_add(eng, dst, Hc, arr, aii, shp4, rb, wb, accumulate_into=None):
        X = shp4[1]
        P_ = scP[:, 0:X, :, :]
        Q_ = scQ[:, 0:X, :, :]
        tt(eng, P_, Hc, arr, ALU.mult, rb + [b_A8], [b_sc])
        yield
        tt(eng, Q_[:, :, :, 0], Hc[:, :, :, 1], aii[:, :, :, 0], ALU.mult, rb + [b_A8], [b_sc])
        yield
        tt(eng, Q_[:, :, :, 1], Hc[:, :, :, 0], aii[:, :, :, 1], ALU.mult, rb + [b_A8], [b_sc])
        yield
        tt(eng, P_, P_, Q_, ALU.add, [b_sc], [b_sc])
        yield
        tt(eng, dst, P_, accumulate_into, ALU.add, [b_sc] + rb, wb)
        yield

    def stageD(b):
        G, L, HLv = b.G, b.L, b.HLv
        eng = "dve"
        if b.kind == "sample":
            for reim, srcd in ((0, s5re0), (1, s5im0)):
                bk, bb_ = pbank()
                dma("sp", h0in, srcd, [], [b_h0in, b_y5, b_y5g], b_h0in)
                pe([(bk[:, pr * 16:pr * 16 + 16], h0in[:, pr * 128:(pr + 1) * 128], identf[0:16, 0:16], True, True)
                    for pr in range(16)], [b_h0in, b_y5, b_y5g, b_identf], [bb_])
                cp("dve", HLv[:, :, 0, :, reim].rearrange("p c r -> p r c"),
                   bk[:, 0:256].rearrange("p (r c) -> p r c", r=16), [bb_], [b_HL])
        else:
            memset("dve", HLv[:, :, 0, :, :], 0.0, [b_HL])
        shp = [128, G, 16, 2]
        arr = APWrr[:, 0, :, :].unsqueeze(1).to_broadcast(shp)
        aii = APWii[:, 0, :, :].unsqueeze(1).to_broadcast(shp)
        for l in range(L):
            yield from cmul_add(eng, HLv[:, :, l + 1, :, :], HLv[:, :, l, :, :], arr, aii, shp, [b_HL], [b_HL],
                                accumulate_into=HLv[:, :, l + 1, :, :])
        if b.kind != "sample":
            shpL = [128, L, 16, 2]
            for g in range(G):
                Hin = Hcar if g == 0 else HLv[:, g - 1, L, :, :]
                rb = [b_Hcar] if g == 0 else [b_HL]
                Hb = Hin.unsqueeze(1).to_broadcast(shpL)
                yield from cmul_add(eng, HLv[:, g, 1:L + 1, :, :], Hb, APWrr[:, 0:L, :, :], APWii[:, 0:L, :, :], shpL,
                                    rb + [b_HL], [b_HL], accumulate_into=HLv[:, g, 1:L + 1, :, :])
                cp("act", HLv[:, g, 0, :, :], Hin, rb, [b_HL])
            cp("act", Hcar, HLv[:, G - 1, L, :, :], [b_HL], [b_Hcar])
        if b.need_out:
            cp("act", Hpb[:, 0:b.nch, :, :].rearrange("p (g l) r i -> p g l (r i)", g=G),
               HLv[:, :, 0:L, :, :].rearrange("p g l r i -> p g l (r i)"), [b_HL], [b_Hpb])
        if b.kind == "sample":
            for reim, dstd in ((0, s5res), (1, s5ims)):
                for q4 in range(4):
                    bk, bb_ = pbank()
                    pe([(bk[0:16, j * 128:(j + 1) * 128], HLv[:, :, L, 4 * q4 + j, reim], identf, True, True)
                        for j in range(4)], [b_HL, b_identf], [bb_])
                    cp("act", s5o[:, q4 * 512:(q4 + 1) * 512], bk[0:16, :], [bb_], [b_s5o, b_y5, b_y5g])
                dma("sp", dstd, s5o, [b_s5o, b_y5, b_y5g], [], b_s5o)

    def stageE(b):
        ntb, nch = b.ntb, b.nch
        uT, b_uT = b.uT, b.b_uT
        for ft in range(4):
            bk, bb_ = pbank()
            mms = []
            for s_ in range(T5):
                for tau in range(s_ + 1):
                    mms.append((bk[:, s_:ntb:T5], KT[:, ft, tau, :], uT[:, ft, (s_ - tau):ntb:T5],
                                len(mms) == 0, False))
                for prl in range(4):
                    pr = 4 * ft + prl
                    hf = prl // 2
                    for reim in range(2):
                        mms.append((bk[64 * hf:64 * hf + 64, s_:ntb:T5], CAT[:, pr, s_, reim, :],
                                    Hpb[:, 0:nch, pr, reim], False, False))
            mms[-1] = mms[-1][:4] + (True,)
            pe(mms, [b_KT, b_uT, b_CAT, b_Hpb], [bb_], skip=True)
            stt("dve", y5[:, ft, 0:ntb], uT[:, ft, 0:ntb], dsk[:, ft:ft + 1], bk[:, 0:ntb], ALU.mult, ALU.add,
                [b_uT, b_vec, bb_], [b_y5])
        yv = y5[:, :, 0:ntb]; gv = y5g[:, :, 0:ntb]; av = tA[:, :, 0:ntb]
        act(av, yv, AF.Square, [b_y5], [b_tA])
        ts("dve", av, av, 0.044715, 1.0, ALU.mult, ALU.add, [b_tA], [b_tA])
        tt("dve", av, av, yv, ALU.mult, [b_tA, b_y5], [b_tA])
        act(av, av, AF.Exp, [b_tA], [b_tA], scale=-2.0 * math.sqrt(2.0 / math.pi))
        act(av, av, AF.Ln, [b_tA], [b_tA], bias=1.0)
        act(av, av, AF.Exp, [b_tA], [b_tA], scale=-1.0)
        tt("dve", gv, yv, av, ALU.mult, [b_tA, b_y5], [b_y5g])
        cp("act", y5gb[:, :, 0:ntb], gv, [b_y5g], [b_y5gb])
        for fo in range(4):
            bk, bb_ = pbank()
            pe([(bk[:, 0:ntb], wglu[:, fi, 128 * fo:128 * fo + 128], y5gb[:, fi, 0:ntb], fi == 0, fi == 3)
                for fi in range(4)], [b_wglu, b_y5gb], [bb_])
            act(tA[:, fo, 0:ntb], bk[:, 0:ntb], AF.Exp, [bb_], [b_tA], scale=-1.0)
        act(av, av, AF.Ln, [b_tA], [b_tA], bias=1.0)
        act(av, av, AF.Exp, [b_tA], [b_tA], scale=-1.0)
        tt("dve", gv, gv, av, ALU.mult, [b_tA, b_y5g], [b_y5g])
        act(sqb[:, :, 0:ntb], gv, AF.Square, [b_y5g], [b_sqb])
        bk, bb_ = pbank()
        pe([(bk[:, 0:ntb], onesb, sqb[:, fi, 0:ntb], fi == 0, fi == 3) for fi in range(4)], [b_onesb, b_sqb], [bb_])
        act(tB[:, 0:ntb], bk[:, 0:ntb], AF.Ln, [bb_, b_eps], [b_tB], bias=eps_t, scale=1.0 / 512)
        act(tB[:, 0:ntb], tB[:, 0:ntb], AF.Exp, [b_tB], [b_tB], scale=-0.5)
        for fi in range(4):
            stt("dve", mixT[:, 4 + fi, 0:ntb], gv[:, fi, :], s5n[:, fi:fi + 1], tB[:, 0:ntb],
                ALU.mult, ALU.mult, [b_y5g, b_vec, b_tB], [b_mixT])

    def stageF(b):
        col = 0
        tAf = tA.rearrange("p a b -> p (a b)")
        junk = y5gb.rearrange("p a b -> p (a b)")
        for ti, (src, ntok) in enumerate(b.tiles):
            c0 = col
            col += ntok
            i = xt_i[0] % 2
            xt_i[0] += 1
            X, bX = xt[i], b_xt[i]
            dma("sp", X[0:ntok, :], src, [], [bX], bX)
            s_, bs = st2[ti % 2], b_st2[ti % 2]
            hbk = []
            for half in range(2):
                bk, bb_ = pbank()
                pe([(bk[0:ntok, :], mixT[:, k, c0:c0 + ntok], wo[:, k, half * 512:(half + 1) * 512], k == 0, k == 7)
                    for k in range(8)], [b_mixT, b_wo], [bb_])
                act(junk[0:ntok, half * 512:(half + 1) * 512], bk[0:ntok, :], AF.Square, [bb_], [b_y5gb, bs],
                    accum=s_[0:ntok, half:half + 1])
                hbk.append((bk, bb_))
            tt("dve", s_[0:ntok, 2:3], s_[0:ntok, 0:1], s_[0:ntok, 1:2], ALU.add, [bs], [bs])
            act(s_[0:ntok, 3:4], s_[0:ntok, 2:3], AF.Ln, [bs, b_eps], [bs], bias=eps_t[0:ntok, :], scale=1.0 / D)
            act(s_[0:ntok, 4:5], s_[0:ntok, 3:4], AF.Exp, [bs], [bs], scale=-0.5)
            for half in range(2):
                bk, bb_ = hbk[half]
                sl_ = slice(half * 512, (half + 1) * 512)
                stt("dve", tAf[0:ntok, sl_], bk[0:ntok, :], s_[0:ntok, 4:5], gpm_row[0:ntok, sl_], ALU.mult, ALU.mult,
                    [bb_, bs, b_gpm_row], [b_tA])
            tt("dve", X[0:ntok, :], X[0:ntok, :], tAf[0:ntok, :], ALU.add, [b_tA, bX], [bX])
            dma("sp", x1d[b.x1row0 + c0:b.x1row0 + c0 + ntok, :], X[0:ntok, :], [bX], [], bX)

    blocks = [mk_block(0, "meta", [(meta, NMETA)], 0)]
    blocks.append(mk_block(len(blocks), "sample", [(xs, 128)], SEQ))
    for blk in range(SEQ // BT):
        tiles = [(xp[blk * BT + j * 128: blk * BT + (j + 1) * 128, :], 128) for j in range(BT // 128)]
        blocks.append(mk_block(len(blocks), "prompt", tiles, blk * BT))
    nb = len(blocks)
    def run_all(g):
        for _ in g:
            pass

    def merge(gens_weights):
        live = [[g, w] for (g, w) in gens_weights]
        while live:
            for item in list(live):
                g, w = item
                for _ in range(w):
                    try:
                        next(g)
                    except StopIteration:
                        live.remove(item)
                        break

    run_all(stageA(blocks[0]))
    for i, b in enumerate(blocks):
        stageC(b)
        if b.kind == "meta":
            run_all(stageB(b))
            run_all(stageD(b))
            if i + 1 < nb:
                run_all(stageA(blocks[i + 1]))
        else:
            gl = [(stageB(b), 1), (stageD(b), 3)]
            if i + 1 < nb:
                gl.append((stageA(blocks[i + 1]), 1))
            merge(gl)
        if b.need_out:
            stageE(b)
            stageF(b)
        if i == nb - 1:
            dma("sp", glap.rearrange("h d v -> d h v"), Sg, [b_Sg], [], b_Sg)
            cp("dve", Hout.rearrange("p i r -> p r i"), Hcar, [b_Hcar], [b_Hout])
            bkh, bbh = pbank()
            pe([(bkh[0:16, 128 * i_:128 * i_ + 128], Hout[:, i_, :], identf, True, True) for i_ in range(2)],
               [b_Hout, b_identf], [bbh])
            cp("dve", tB[0:16, 0:256], bkh[0:16, 0:256], [bbh], [b_tB])
            dma("sp", s5rep, tB[0:16, 0:128], [b_tB], [], b_tB)
            dma("sp", s5imp, tB[0:16, 128:256], [b_tB], [], b_tB)
    S.barrier()

    AR.reset(PERSIST_MARK)
    gpf_row = AR.alloc([D], F32); b_gpf_row = Buf("gpf_row")
    dma("sp", gpf_row, g_post_ffn.partition_broadcast(128), [], [b_gpf_row], b_gpf_row)
    wg = AR.alloc([8, DFF], BF16)
    wu = AR.alloc([8, DFF], BF16)
    wd = AR.alloc([NFF, D], BF16)
    GT = 512
    h2T = AR.alloc([8, GT], BF16); b_h2T = Buf("h2T")
    actT = AR.alloc([NFF, GT], BF16); b_actT = Buf("actT")
    x1r = [AR.alloc([D], F32) for _ in range(2)]; b_x1r = [Buf("x1r0"), Buf("x1r1")]
    sgt = [AR.alloc([GT], F32) for _ in range(2)]; b_sgt = [Buf("sgt0"), Buf("sgt1")]
    otmp = AR.alloc([D], F32); b_otmp = Buf("otmp")
    fhn = [AR.alloc([D], BF16) for _ in range(4)]; b_fhn = [Buf("fhn%d" % i) for i in range(4)]
    fst = [AR.alloc([8], F32) for _ in range(4)]; b_fst = [Buf("fst%d" % i) for i in range(4)]
    fst2 = [AR.alloc([8], F32) for _ in range(2)]; b_fst2 = [Buf("fst20"), Buf("fst21")]

    b_wgh = [Buf("wgh0"), Buf("wgh1")]
    b_wuh = [Buf("wuh0"), Buf("wuh1")]
    b_wdh = Buf("wdh")
    def issue_ffn_weight_loads():
        wgv = wg_b.rearrange("(kt p) c -> p kt c", p=128)
        wuv = wu_b.rearrange("(kt p) c -> p kt c", p=128)
        wdv = wd_b.rearrange("(kt p) c -> p kt c", p=128)
        for hh in range(2):
            cs_ = slice(hh * 1408, (hh + 1) * 1408)
            dma("sp", wg[:, :, cs_], wgv[:, :, cs_], [b_cvg], [b_wgh[hh]], b_wgh[hh])
            dma("act", wu[:, :, cs_], wuv[:, :, cs_], [b_cvu], [b_wuh[hh]], b_wuh[hh])
        dma("sp", wd[:, 0:11, :], wdv[:, 0:11, :], [b_cvd], [b_wdh], b_wdh)
        dma("act", wd[:, 11:22, :], wdv[:, 11:22, :], [b_cvd], [b_wdh], b_wdh)

    groups = [(g * GT, GT, yp, g * GT) for g in range(SEQ // GT)] + [(SEQ, 128, ys, 0)]
    xi = [0]

    def prepA(grp):
        (row0, ng, ydst, yrow0) = grp
        for j in range(ng // 128):
            X1, bX1 = x1r[xi[0] % 2], b_x1r[xi[0] % 2]
            xi[0] += 1
            H_, bH = fhn[j], b_fhn[j]
            s_, bs = fst[j], b_fst[j]
            dma("sp", X1, x1d[row0 + j * 128:row0 + (j + 1) * 128, :], [], [bX1], bX1)
            act(H_, X1, AF.Square, [bX1], [bH, bs], accum=s_[:, 0:1])
            act(s_[:, 1:2], s_[:, 0:1], AF.Ln, [bs, b_eps], [bs], bias=eps_t, scale=1.0 / D)
            act(s_[:, 2:3], s_[:, 1:2], AF.Exp, [bs], [bs], scale=-0.5)
            S.op("act", lambda e, o=H_, i_=X1, m_=s_[:, 2:3]: e.mul(o, i_, m_), [bX1, bs], [bH])

    def prepB(grp):
        (row0, ng, ydst, yrow0) = grp
        for j in range(ng // 128):
            H_, bH = fhn[j], b_fhn[j]
            for half in range(2):
                bk, bb_ = pbank()
                pe([(bk[:, q * 128:(q + 1) * 128], H_[:, (half * 4 + q) * 128:(half * 4 + q + 1) * 128], identb, True, True)
                    for q in range(4)], [bH, b_identb], [bb_])
                for q in range(4):
                    kt_ = half * 4 + q
                    if q % 2 == 0:
                        S.op("act", lambda e, o=h2T[:, kt_, j * 128:(j + 1) * 128], i_=bk[:, q * 128:(q + 1) * 128],
                             m_=gpf_col[:, kt_:kt_ + 1]: e.mul(o, i_, m_), [bb_, b_gcol], [b_h2T])
                    else:
                        ts("dve", h2T[:, kt_, j * 128:(j + 1) * 128], bk[:, q * 128:(q + 1) * 128],
                           gpf_col[:, kt_:kt_ + 1], None, ALU.mult, None, [bb_, b_gcol], [b_h2T])

    def gate_up(grp):
        (row0, ng, ydst, yrow0) = grp
        for ff in range(NFF):
            bkg, bbg = pbank()
            pe([(bkg[:, 0:ng], wg[:, kt, ff * 128:(ff + 1) * 128], h2T[:, kt, 0:ng], kt == 0, kt == 7) for kt in range(8)],
               [b_wgh[ff // 11], b_h2T], [bbg])
            bku, bbu = pbank()
            pe([(bku[:, 0:ng], wu[:, kt, ff * 128:(ff + 1) * 128], h2T[:, kt, 0:ng], kt == 0, kt == 7) for kt in range(8)],
               [b_wuh[ff // 11], b_h2T], [bbu])
            sgl, bsgl = sgt[ff % 2], b_sgt[ff % 2]
            act(sgl[:, 0:ng], bkg[:, 0:ng], AF.Silu, [bbg], [bsgl])
            tt("dve", actT[:, ff, 0:ng], sgl[:, 0:ng], bku[:, 0:ng], ALU.mult, [bsgl, bbu], [b_actT])

    def down_tail(grp):
        (row0, ng, ydst, yrow0) = grp
        for j in range(ng // 128):
            s_, bs = fst2[j % 2], b_fst2[j % 2]
            X1, bX1 = x1r[xi[0] % 2], b_x1r[xi[0] % 2]
            xi[0] += 1
            dma("sp", X1, x1d[row0 + j * 128:row0 + (j + 1) * 128, :], [], [bX1], bX1)
            hbk = []
            for half in range(2):
                bk, bb_ = pbank()
                pe([(bk, actT[:, ff, j * 128:(j + 1) * 128], wd[:, ff, half * 512:(half + 1) * 512], ff == 0, ff == NFF - 1)
                    for ff in range(NFF)], [b_actT, b_wdh], [bb_])
                act(sgt[half][:, 0:512], bk, AF.Square, [bb_], [b_sgt[half], bs], accum=s_[:, half:half + 1])
                hbk.append((bk, bb_))
            tt("dve", s_[:, 2:3], s_[:, 0:1], s_[:, 1:2], ALU.add, [bs], [bs])
            act(s_[:, 3:4], s_[:, 2:3], AF.Ln, [bs, b_eps], [bs], bias=eps_t, scale=1.0 / D)
            act(s_[:, 4:5], s_[:, 3:4], AF.Exp, [bs], [bs], scale=-0.5)
            for half in range(2):
                bk, bb_ = hbk[half]
                sl_ = slice(half * 512, (half + 1) * 512)
                stt("dve", otmp[:, sl_], bk, s_[:, 4:5], gpf_row[:, sl_], ALU.mult, ALU.mult, [bb_, bs, b_gpf_row], [b_otmp])
            tt("dve", X1, X1, otmp, ALU.add, [b_otmp, bX1], [bX1])
            dma("sp", ydst[yrow0 + j * 128:yrow0 + (j + 1) * 128, :], X1, [bX1], [], bX1)

    prepA(groups[0])
    prepB(groups[0])
    issue_ffn_weight_loads()
    for gi, grp in enumerate(groups):
        gate_up(grp)
        if gi + 1 < len(groups):
            prepA(groups[gi + 1])
        down_tail(grp)
        if gi + 1 < len(groups):
            prepB(groups[gi + 1])

    S.barrier()
    S.emit()
    es.close()
    return nc


_NC_CACHE = {}


def kernel(**inp):
    f = lambda a: np.ascontiguousarray(np.asarray(a, dtype=np.float32))
    if "nc" not in _NC_CACHE:
        _NC_CACHE["nc"] = build_nc()
    nc = _NC_CACHE["nc"]
    shared = {
        "meta": f(inp["meta_tokens"]),
        "g_pre_mix": f(inp["g_pre_mix"][0]), "w_in": f(inp["w_in"][0]), "w_gk2": f(inp["w_gk2"][0]),
        "b_gk": f(inp["b_gk"][0]).reshape(1, 256), "gla_norm": f(inp["gla_norm"][0]).reshape(128, 1),
        "s5_a_re": f(inp["s5_a_re"][0]).reshape(2048), "s5_a_im": f(inp["s5_a_im"][0]).reshape(2048),
        "s5_b_re": f(inp["s5_b_re"][0]).reshape(2048, 16), "s5_b_im": f(inp["s5_b_im"][0]).reshape(2048, 16),
        "s5_c_re": f(inp["s5_c_re"][0]), "s5_c_im": f(inp["s5_c_im"][0]),
        "s5_d": f(inp["s5_d"][0]), "s5_log_dt": f(inp["s5_log_dt"][0]).reshape(1, 32),
        "w_s5_glu": f(inp["w_s5_glu"][0]), "s5_norm": f(inp["s5_norm"][0]),
        "w_o": f(inp["w_o"][0]), "g_post_mix": f(inp["g_post_mix"][0]).reshape(1, D),
        "g_pre_ffn": f(inp["g_pre_ffn"][0]),
        "w_gate": f(inp["w_gate"][0]), "w_up": f(inp["w_up"][0]), "w_down": f(inp["w_down"][0]),
        "g_post_ffn": f(inp["g_post_ffn"][0]).reshape(1, D),
    }
    in_maps = []
    for c in range(NCORES):
        m = dict(shared)
        m["xp"] = f(inp["x_prompt"][c])
        m["xs"] = f(inp["x_sample"][16 * c:16 * c + 16]).reshape(128, D)
        m["sgla"] = f(inp["state_gla"][0, 16 * c:16 * c + 16])
        m["s5re0"] = f(inp["state_s5_re"][0, 16 * c:16 * c + 16]).reshape(16, 2048)
        m["s5im0"] = f(inp["state_s5_im"][0, 16 * c:16 * c + 16]).reshape(16, 2048)
        in_maps.append(m)
    res = run_bass_kernel_spmd(nc, in_maps, core_ids=list(range(NCORES)))
    r = res.results
    y_prompt = np.stack([r[c]["yp"] for c in range(NCORES)]).astype(np.float32)
    y_sample = np.concatenate([r[c]["ys"].reshape(16, 8, D) for c in range(NCORES)]).astype(np.float32)
    gla_p = np.stack([r[c]["glap"] for c in range(NCORES)])[None].astype(np.float32)
    s5re_p = np.stack([r[c]["s5rep"].reshape(32, 64) for c in range(NCORES)])[None].astype(np.float32)
    s5im_p = np.stack([r[c]["s5imp"].reshape(32, 64) for c in range(NCORES)])[None].astype(np.float32)
    gla_s = np.concatenate([r[c]["glas"] for c in range(NCORES)])[None].astype(np.float32)
    s5re_s = np.concatenate([r[c]["s5res"].reshape(16, 32, 64) for c in range(NCORES)])[None].astype(np.float32)
    s5im_s = np.concatenate([r[c]["s5ims"].reshape(16, 32, 64) for c in range(NCORES)])[None].astype(np.float32)
    return (y_prompt, y_sample, gla_p, s5re_p, s5im_p, gla_s, s5re_s, s5im_s)
```
